# Optimizing a Trainium2 kernel written in Bass

```python
import math
import jax, jax.numpy as jnp
from jax import lax
import numpy as np

D_MODEL = 4096
BATCH = 4
SEQ = 4096
DEPTH = 1

HEAD_DIM = 128
N_DIFF_HEADS = D_MODEL // (4 * HEAD_DIM)
N_FOX_HEADS = D_MODEL // (2 * HEAD_DIM)
DIFF_WIDTH = N_DIFF_HEADS * 2 * HEAD_DIM
FOX_WIDTH = N_FOX_HEADS * HEAD_DIM
MIX_WIDTH = DIFF_WIDTH + FOX_WIDTH
IN_COLS = 3 * DIFF_WIDTH + 3 * FOX_WIDTH + N_FOX_HEADS
D_FF = ((8 * D_MODEL + 3 * 256 - 1) // (3 * 256)) * 256
NUM_BUCKETS = 32
MAX_DISTANCE = 128
Q_BLOCK = 128
EPS = 1e-6
NEG_INF = -1e30

kernel_name = "hybrid_diffattn_fox_parallel_heads"


def rmsnorm(x, g):
    xf = x.astype(jnp.float32)
    y = xf * lax.rsqrt(jnp.mean(xf * xf, axis=-1, keepdims=True) + EPS)
    return (y * g.astype(jnp.float32)).astype(x.dtype)


def lambda_init(layer_idx):
    return 0.8 - 0.6 * math.exp(-0.3 * layer_idx)


def t5_causal_bucket(n):
    max_exact = NUM_BUCKETS // 2
    nf = jnp.maximum(n, 1).astype(jnp.float32)
    large = max_exact + (jnp.log(nf / max_exact) / math.log(MAX_DISTANCE / max_exact)
                         * (NUM_BUCKETS - max_exact)).astype(jnp.int32)
    large = jnp.minimum(large, NUM_BUCKETS - 1)
    return jnp.where(n < max_exact, n, large)


def hybrid_mixer(h, w_in, b_f, lam, rel_bias_table, diff_subln_g, lam_init, w_o):
    B, S, _ = h.shape
    proj = h @ w_in
    o = 0
    dq = proj[..., o:o + DIFF_WIDTH]; o += DIFF_WIDTH
    dk = proj[..., o:o + DIFF_WIDTH]; o += DIFF_WIDTH
    dv = proj[..., o:o + DIFF_WIDTH]; o += DIFF_WIDTH
    fq = proj[..., o:o + FOX_WIDTH]; o += FOX_WIDTH
    fk = proj[..., o:o + FOX_WIDTH]; o += FOX_WIDTH
    fv = proj[..., o:o + FOX_WIDTH]; o += FOX_WIDTH
    f_logit = proj[..., o:o + N_FOX_HEADS] + b_f

    dq = dq.reshape(B, S, N_DIFF_HEADS, 2, HEAD_DIM).transpose(0, 2, 3, 1, 4)
    dk = dk.reshape(B, S, N_DIFF_HEADS, 2, HEAD_DIM).transpose(0, 2, 3, 1, 4)
    dv = dv.reshape(B, S, N_DIFF_HEADS, 2 * HEAD_DIM).transpose(0, 2, 1, 3)
    fq = fq.reshape(B, S, N_FOX_HEADS, HEAD_DIM).transpose(0, 2, 1, 3)
    fk = fk.reshape(B, S, N_FOX_HEADS, HEAD_DIM).transpose(0, 2, 1, 3)
    fv = fv.reshape(B, S, N_FOX_HEADS, HEAD_DIM).transpose(0, 2, 1, 3)
    log_f = jax.nn.log_sigmoid(f_logit.astype(jnp.float32))
    cum = jnp.cumsum(log_f, axis=1).transpose(0, 2, 1)

    scale = HEAD_DIM ** -0.5
    k_pos = jnp.arange(S, dtype=jnp.int32)
    n_blocks = S // Q_BLOCK

    def block(i):
        start = i * Q_BLOCK
        q_pos = start + jnp.arange(Q_BLOCK, dtype=jnp.int32)
        dist = q_pos[:, None] - k_pos[None, :]
        causal = dist >= 0
        bias = rel_bias_table[t5_causal_bucket(jnp.maximum(dist, 0))]
        bias = bias.astype(jnp.float32).transpose(2, 0, 1)

        q_d = lax.dynamic_slice_in_dim(dq, start, Q_BLOCK, axis=3)
        s_d = jnp.einsum('bhmqd,bhmkd->bhmqk', q_d, dk).astype(jnp.float32) * scale
        s_d = s_d + bias[None, :, None]
        s_d = jnp.where(causal, s_d, NEG_INF)
        p_d = jax.nn.softmax(s_d, axis=-1)
        a_d = p_d[:, :, 0] - lam * p_d[:, :, 1]
        o_d = jnp.einsum('bhqk,bhkd->bhqd', a_d.astype(dv.dtype), dv)

        q_f = lax.dynamic_slice_in_dim(fq, start, Q_BLOCK, axis=2)
        c_q = lax.dynamic_slice_in_dim(cum, start, Q_BLOCK, axis=2)
        s_f = jnp.einsum('bhqd,bhkd->bhqk', q_f, fk).astype(jnp.float32) * scale
        s_f = s_f + (c_q[..., :, None] - cum[..., None, :])
        s_f = jnp.where(causal, s_f, NEG_INF)
        p_f = jax.nn.softmax(s_f, axis=-1)
        o_f = jnp.einsum('bhqk,bhkd->bhqd', p_f.astype(fv.dtype), fv)
        return o_d, o_f

    o_d, o_f = lax.map(block, jnp.arange(n_blocks))
    o_d = o_d.transpose(1, 0, 3, 2, 4).reshape(B, S, N_DIFF_HEADS, 2 * HEAD_DIM)
    o_f = o_f.transpose(1, 0, 3, 2, 4).reshape(B, S, N_FOX_HEADS, HEAD_DIM)
    o_d = rmsnorm(o_d, diff_subln_g) * (1.0 - lam_init)
    mixed = jnp.concatenate([o_d.reshape(B, S, DIFF_WIDTH),
                             o_f.reshape(B, S, FOX_WIDTH)], axis=-1)
    return mixed @ w_o


def swiglu(h, w_gate, w_up, w_down):
    return (jax.nn.silu(h @ w_gate) * (h @ w_up)) @ w_down


def setup_inputs(seed: int = 0) -> dict:
    key = jax.random.key(seed)
    ks = jax.random.split(key, 20)
    f32 = jnp.float32
    nrm = lambda k, shape, s: jax.random.normal(k, shape, f32) * s
    return {
        "x": nrm(ks[0], (BATCH, SEQ, D_MODEL), 1.0),
        "attn_norm_g": 1.0 + nrm(ks[1], (DEPTH, D_MODEL), 0.02),
        "w_in": nrm(ks[2], (DEPTH, D_MODEL, IN_COLS), D_MODEL ** -0.5),
        "b_f": 3.0 + nrm(ks[3], (DEPTH, N_FOX_HEADS), 0.5),
        "lambda_q1": nrm(ks[4], (DEPTH, HEAD_DIM), 0.1),
        "lambda_k1": nrm(ks[5], (DEPTH, HEAD_DIM), 0.1),
        "lambda_q2": nrm(ks[6], (DEPTH, HEAD_DIM), 0.1),
        "lambda_k2": nrm(ks[7], (DEPTH, HEAD_DIM), 0.1),
        "rel_bias_table": nrm(ks[8], (NUM_BUCKETS, N_DIFF_HEADS), 0.5),
        "diff_subln_g": 1.0 + nrm(ks[9], (DEPTH, 2 * HEAD_DIM), 0.02),
        "w_o": nrm(ks[10], (DEPTH, MIX_WIDTH, D_MODEL), MIX_WIDTH ** -0.5),
        "ffn_norm_g": 1.0 + nrm(ks[11], (DEPTH, D_MODEL), 0.02),
        "w_gate": nrm(ks[12], (DEPTH, D_MODEL, D_FF), D_MODEL ** -0.5),
        "w_up": nrm(ks[13], (DEPTH, D_MODEL, D_FF), D_MODEL ** -0.5),
        "w_down": nrm(ks[14], (DEPTH, D_FF, D_MODEL), D_FF ** -0.5),
        "final_norm_g": 1.0 + nrm(ks[15], (D_MODEL,), 0.02),
    }


def reference(x, attn_norm_g, w_in, b_f, lambda_q1, lambda_k1, lambda_q2, lambda_k2,
              rel_bias_table, diff_subln_g, w_o, ffn_norm_g, w_gate, w_up, w_down,
              final_norm_g):
    for l in range(DEPTH):
        lam_init = lambda_init(l)
        lam = (jnp.exp(jnp.sum(lambda_q1[l].astype(jnp.float32) * lambda_k1[l].astype(jnp.float32)))
               - jnp.exp(jnp.sum(lambda_q2[l].astype(jnp.float32) * lambda_k2[l].astype(jnp.float32)))
               + lam_init)
        h = rmsnorm(x, attn_norm_g[l])
        x = x + hybrid_mixer(h, w_in[l], b_f[l], lam, rel_bias_table, diff_subln_g[l],
                             lam_init, w_o[l])
        h = rmsnorm(x, ffn_norm_g[l])
        x = x + swiglu(h, w_gate[l], w_up[l], w_down[l])
    return rmsnorm(x, final_norm_g)
```

```python
import math
import os
from contextlib import ExitStack

import numpy as np
import concourse.bass as bass
import concourse.mybir as mybir
from concourse.bass_utils import run_bass_kernel_spmd

F32 = mybir.dt.float32
BF16 = mybir.dt.bfloat16
AF = mybir.ActivationFunctionType
ALU = mybir.AluOpType
AX = mybir.AxisListType

EPS = 1e-6
NUM_BUCKETS = 32
MAX_DISTANCE = 128
NEG = -30000.0
LAM_INIT = 0.8 - 0.6 * math.exp(-0.3 * 0)


class Cfg:
    def __init__(s, D=4096, S=4096, DFF=11008):
        s.D = D; s.S = S; s.DFF = DFF
        s.NDH = D // 512; s.NFH = D // 256
        s.DW = s.NDH * 256; s.FW = s.NFH * 128
        s.INC = 3 * s.DW + 3 * s.FW + s.NFH
        s.NCH = D // 128; s.NB = S // 128; s.NOWN = S // 2
        s.NFC = DFF // 128
        s.NHM = 2 * s.NDH + s.NFH
        s.NBA = 8
        s.NTA = S // (128 * s.NBA)
        s.NTC = s.NOWN // 512
        s.NG = s.NOWN // 512
        s.PC = 256
        s.KC = min(16, s.NCH)
        s.SCALE = 128 ** -0.5


class Sem:
    pass


class Builder:
    ENG = ("pe", "act", "dve", "pool", "sp")

    def __init__(self, nc, es):
        self.nc = nc; self.es = es
        self.q = {e: [] for e in self.ENG}
        self.sems = []
        self.waited = {e: {} for e in self.ENG}
        self.seq = {}

    def sem(self, name):
        s = Sem(); s.h = self.es.enter_context(self.nc.semaphore(name)); s.n = 0; s.name = name
        self.sems.append(s)
        return s

    def _ws(self, eng, waits):
        ws = []
        wd = self.waited[eng]
        for w in waits:
            if w is None:
                continue
            s, t = w
            if t is None or t <= 0:
                continue
            if wd.get(s.name, 0) >= t:
                continue
            wd[s.name] = t
            ws.append((s.h, t))
        return ws

    def op(self, eng, fn, waits=(), sig=None, step=1, indep=False, is_dma=False):
        if eng in ("act", "dve") and not is_dma:
            if eng not in self.seq:
                self.seq[eng] = self.sem("seq_" + eng)
            sq = self.seq[eng]
            waits = list(waits)
            if sq.n > 0 and not (indep and not os.environ.get('NOINDEP')):
                waits.append((sq, sq.n))
            sig = sq
            step = 1
        ws = self._ws(eng, waits)
        ticket = None
        sh = None
        if sig is not None:
            sig.n += step; ticket = sig.n; sh = sig.h

        def run(e):
            for h, t in ws:
                e.wait_ge(h, t)
            ins = fn(e)
            if sh is not None:
                ins.then_inc(sh, step)
        self.q[eng].append(run)
        return (sig, ticket) if sig is not None else None

    def dma(self, eng, out, in_, waits=(), sig=None):
        return self.op(eng, lambda e: e.dma_start(out=out, in_=in_), waits, sig, step=16, is_dma=True)

    def wait(self, eng, waits):
        ws = self._ws(eng, waits)
        if ws:
            def run(e):
                for h, t in ws:
                    e.wait_ge(h, t)
            self.q[eng].append(run)

    def barrier(self):
        for eng in self.ENG:
            self.wait(eng, [(s, s.n) for s in self.sems if s.n > 0])

    def finish(self):
        nc = self.nc
        with nc.Block() as block:
            q = self.q

            @block.tensor
            def _(e):
                for f in q["pe"]:
                    f(e)

            @block.scalar
            def _(e):
                for f in q["act"]:
                    f(e)

            @block.vector
            def _(e):
                for f in q["dve"]:
                    f(e)

            @block.gpsimd
            def _(e):
                for f in q["pool"]:
                    f(e)

            @block.sync
            def _(e):
                for f in q["sp"]:
                    f(e)


def build(cfg, debug=False):
    c = cfg
    D, S, NCH, NB, NOWN = c.D, c.S, c.NCH, c.NB, c.NOWN
    NDH, NFH, NHM, NFC = c.NDH, c.NFH, c.NHM, c.NFC
    nc = bass.Bass("TRN2", target_bir_lowering=False)

    def din(name, shape, dt=F32):
        return nc.dram_tensor(name, list(shape), dt, kind="ExternalInput").ap()

    x_loc = din("x_loc", [S, D])
    x_own = din("x_own", [NOWN, D])
    w_in = din("w_in", [D, c.INC])
    w_o = din("w_o", [D, D])
    w_gate = din("w_gate", [D, c.DFF])
    w_up = din("w_up", [D, c.DFF])
    w_down = din("w_down", [c.DFF, D])
    g_attn = din("g_attn", [128, D])
    g_ffn = din("g_ffn", [128, D])
    g_fin = din("g_fin", [128, D])
    g_sub = din("g_sub", [128, 256])
    bf_bc = din("bf_bc", [128, NFH])
    lam_bc = din("lam_bc", [128, 4 * 128])
    tab_bc = din("tab_bc", [128, NUM_BUCKETS * NDH])
    tri_in = din("tri", [128, 128])
    ohm_in = din("ohm", [128, (NUM_BUCKETS + 1) * 128])
    sel_in = din("sel", [128, NFH * 128])
    ident_in = din("ident", [128, 128])

    okind = "ExternalOutput"
    out_d = nc.dram_tensor("out", [NOWN, D], F32, kind=okind).ap()
    skind = "ExternalOutput" if debug else "Internal"
    VC = c.DW + c.FW
    KT_d = nc.dram_tensor("KT_d", [NHM, 128, S], BF16, kind=skind).ap()
    QT_d = nc.dram_tensor("QT_d", [NHM, 128, NOWN], BF16, kind=skind).ap()
    V_d = nc.dram_tensor("V_d", [S, VC], BF16, kind=skind).ap()
    mixT_d = nc.dram_tensor("mixT_d", [NCH, 128, NOWN], BF16, kind=skind).ap()
    x1_d = nc.dram_tensor("x1_d", [NOWN, D], F32, kind=skind).ap()
    cum_d = nc.dram_tensor("cum_d", [128, NB * NFH], F32, kind=skind).ap() if debug else None

    es = ExitStack()
    with es:
        B = Builder(nc, es)

        def sb(name, shape, dt):
            return es.enter_context(nc.sbuf_tensor(name, list(shape), dt))

        NS = 4
        RING_E = 8192
        ring = [sb(f"ring{i}", [128, RING_E], BF16) for i in range(NS)]
        BIG_E = 70656
        big = sb("big", [128, BIG_E], BF16)
        ident = sb("ident_sb", [128, 128], BF16)
        smallf = sb("smallf", [128, 64], F32)
        zero_c = smallf[:, 0:1]
        eps_c = smallf[:, 1:2]
        neglam = smallf[:, 2:3]
        ssb = smallf[:, 4:8]
        rsb = smallf[:, 8:12]
        lsc = smallf[:, 12:20]
        rlb = smallf[:, 20:28]
        nlb = smallf[:, 28:36]

        class Carver:
            def __init__(self, base=0):
                self.off = base

            def take(self, nbytes, dt, shape=None):
                nbytes = (nbytes + 3) // 4 * 4
                assert self.off % 4 == 0
                ne = nbytes // 2
                a = big[:, self.off // 2: self.off // 2 + ne]
                self.off += nbytes
                assert self.off <= BIG_E * 2, self.off
                if dt == F32:
                    a = a.bitcast(F32)
                return a

        TOP_BASE = BIG_E * 2 - 18432
        top = Carver(base=TOP_BASE)
        cumL = top.take(NB * NFH * 4, F32)
        gqT = top.take(NOWN * 2, BF16)
        sel = top.take(NFH * 128 * 2, BF16)
        tri = top.take(128 * 4, F32)
        ones_f = top.take(128 * 4, F32)
        ident_f = top.take(128 * 4, F32)
        tab = top.take(NUM_BUCKETS * NDH * 4, F32)
        gsub = top.take(256 * 4, F32)
        bfb = top.take(NFH * 4, F32)
        runL = top.take(NFH * 4, F32)
        Wf_sb = top.take(NCH * NFH * 2, BF16)
        lamv = top.take(4 * 128 * 4, F32)
        assert top.off <= BIG_E * 2

        P = [es.enter_context(nc.psum_tensor(f"P{i}", [128, 512], F32)) for i in range(6)]
        PT0 = es.enter_context(nc.psum_tensor("PT0", [128, 1024], BF16))
        P6 = es.enter_context(nc.psum_tensor("P6", [128, 512], F32))
        PT = [PT0[:, :], P6[:, :].bitcast(BF16)]
        SB = [P[0], P[1], P6]
        NSB = 3

        ring_ld = [B.sem(f"ring_ld{i}") for i in range(NS)]
        pe_piece = B.sem("pe_piece")
        st = {"piece": 0, "piece_tk": []}

        piece_srcs = []

        def req_piece(src_ap, shape3):
            i = st["piece"]; st["piece"] += 1
            s = i % NS
            a, b = shape3
            dst = ring[s][:, 0:a * b].rearrange("p (a b) -> p a b", a=a)
            waits = []
            if i >= NS:
                waits.append(st["piece_tk"][i - NS])
            tk = B.dma("pool", dst, src_ap, waits=waits, sig=ring_ld[s])
            return dst, tk

        def piece_done_sig():
            return pe_piece

        def piece_consumed(tk):
            st["piece_tk"].append(tk)


        setup_ld = B.sem("setup_ld")
        tks = []
        tks.append(B.dma("sp", tri, tri_in, sig=setup_ld))
        tks.append(B.dma("sp", ident_f, ident_in, sig=setup_ld))
        tks.append(B.dma("sp", tab, tab_bc, sig=setup_ld))
        tks.append(B.dma("sp", gsub, g_sub, sig=setup_ld))
        tks.append(B.dma("sp", bfb, bf_bc, sig=setup_ld))
        tks.append(B.dma("sp", lamv, lam_bc, sig=setup_ld))
        setup_c = B.sem("setup_c")
        tks.append(B.dma("pool", ident[:], ident_in, sig=setup_c))
        tks.append(B.dma("pool", sel[:, :], sel_in, sig=setup_c))
        wf_src = w_in[:, c.INC - NFH: c.INC].rearrange("(c p) f -> p c f", p=128)
        tks.append(B.dma("pool", Wf_sb.rearrange("p (c f) -> p c f", c=NCH), wf_src, sig=setup_c))
        setup_tk = [tks[5], tks[8]]

        dve_setup = B.sem("dve_setup")
        act_setup = B.sem("act_setup")
        B.op("dve", lambda e: e.memset(smallf[:, :], 0.0), waits=setup_tk)
        B.op("dve", lambda e: e.memset(eps_c, EPS))
        B.op("dve", lambda e: e.memset(ones_f, 1.0))
        B.op("dve", lambda e: e.memset(runL, 0.0))
        B.op("dve", lambda e: e.memset(gqT, 0.0))
        B.op("dve", lambda e: e.tensor_scalar(out=gsub, in0=gsub, scalar1=1.0 - LAM_INIT, scalar2=None, op0=ALU.mult))
        lv = lamv.rearrange("p (a d) -> p a d", a=4)
        junk128 = top.take(128 * 4, F32)
        B.op("dve", lambda e: e.scalar_tensor_tensor(out=junk128, in0=lv[:, 0, :], scalar=1.0, in1=lv[:, 1, :],
                                                     op0=ALU.mult, op1=ALU.mult, accum_out=lsc[:, 0:1]))
        tk_l = B.op("dve", lambda e: e.scalar_tensor_tensor(out=junk128, in0=lv[:, 2, :], scalar=1.0, in1=lv[:, 3, :],
                                                            op0=ALU.mult, op1=ALU.mult, accum_out=lsc[:, 1:2]),
                    sig=dve_setup)
        tk_e = B.op("act", lambda e: e.activation(out=lsc[:, 2:4], in_=lsc[:, 0:2], func=AF.Exp), waits=[tk_l],
                    sig=act_setup)
        B.op("dve", lambda e: e.tensor_tensor(out=lsc[:, 4:5], in0=lsc[:, 3:4], in1=lsc[:, 2:3], op=ALU.subtract),
             waits=[tk_e])
        tk_setup_dve = B.op("dve", lambda e: e.tensor_scalar(out=neglam, in0=lsc[:, 4:5], scalar1=-LAM_INIT, scalar2=None,
                                                             op0=ALU.add), sig=dve_setup)

        dve_ss = B.sem("dve_ss")
        act_rs = B.sem("act_rs")

        def rstd_act(ss_col, rs_col, inv_n, tk_ss):
            B.op("act", lambda e: e.activation(out=ss_col, in_=ss_col, func=AF.Ln, bias=eps_c, scale=inv_n), waits=[tk_ss])
            return B.op("act", lambda e: e.activation(out=rs_col, in_=ss_col, func=AF.Exp, scale=-0.5), sig=act_rs)

        xs_ld = [B.sem("xs_ld0"), B.sem("xs_ld1")]
        dve_norm = B.sem("dve_norm")
        pe_tp = B.sem("pe_tp")
        act_tp = B.sem("act_tp")
        nst = {"n": 0, "norm_tk": [], "tp_last": [], "tpg": 0, "tp_ev": []}

        def norm_transpose(xs, hn, g_tile, src_rows, hT_dst, extra_waits=(), g_wait=None):
            n = nst["n"]; nst["n"] += 1
            b = n % 2
            w = list(extra_waits)
            back = 1 if xs[0] is xs[1] else 2
            if n >= back:
                w.append(nst["norm_tk"][n - back])
            tk_ld = B.dma("sp", xs[b], src_rows, waits=w, sig=xs_ld[b])
            w = [tk_ld, tk_setup_dve]
            if n >= 2:
                w.append(nst["tp_last"][n - 2])
            tk_ss = B.op("dve", lambda e: e.scalar_tensor_tensor(out=hn[b], in0=xs[b], scalar=1.0, in1=xs[b], op0=ALU.mult,
                                                                 op1=ALU.mult, accum_out=ssb[:, b:b + 1]), waits=w, sig=dve_ss)
            tk_rs = rstd_act(ssb[:, b:b + 1], rsb[:, b:b + 1], 1.0 / D, tk_ss)
            tk_n = B.op("dve", lambda e: e.scalar_tensor_tensor(out=hn[b], in0=xs[b], scalar=rsb[:, b:b + 1], in1=g_tile,
                                                                op0=ALU.mult, op1=ALU.mult),
                        waits=[g_wait, tk_rs], sig=dve_norm)
            nst["norm_tk"].append(tk_n)
            last = None
            ev = None
            for c0 in range(0, NCH, 8):
                ng = min(8, NCH - c0)
                gi = nst["tpg"]; nst["tpg"] += 1
                pb = PT[gi % 2]
                w = [tk_n]
                if gi >= 2:
                    w.append(nst["tp_ev"][gi - 2])
                for k in range(ng):
                    cc = c0 + k
                    is_last = (k == ng - 1)
                    fn = (lambda e, cc=cc, k=k, pb=pb: e.transpose(out=pb[:, k * 128:(k + 1) * 128],
                                                                    in_=hn[b][:, cc * 128:(cc + 1) * 128], identity=ident[:]))
                    if is_last:
                        last = B.op("pe", fn, waits=w if k == 0 else (), sig=pe_tp)
                    else:
                        B.op("pe", fn, waits=w if k == 0 else ())
                dst = hT_dst(c0, ng)
                src = pb[:, 0:ng * 128].rearrange("p (a b) -> p a b", a=ng)
                ev = B.op("act", lambda e, dst=dst, src=src: e.copy(out=dst, in_=src), waits=[last], sig=act_tp)
                nst["tp_ev"].append(ev)
            nst["tp_last"].append(last)
            return ev

        ca = Carver()
        NBA = c.NBA
        TA = NBA * 128
        hT = ca.take(NCH * TA * 2, BF16)
        hT4 = hT.rearrange("p (c t q) -> p c t q", c=NCH, t=NBA)
        xsA0 = ca.take(D * 4, F32)
        xsA = [xsA0, xsA0]
        hnA = [ca.take(D * 2, BF16) for _ in range(2)]
        gA = ca.take(D * 4, F32)
        NSTG = 3
        stgK = [ca.take(512 * 2, BF16) for _ in range(NSTG)]
        zf = ca.take(NFH * 4, F32)
        ef = ca.take(NFH * 4, F32)
        lf = ca.take(NFH * 4, F32)
        assert ca.off <= TOP_BASE, ca.off

        gA_ld = B.sem("gA_ld")
        tk_gA = B.dma("sp", gA, g_attn, sig=gA_ld)

        pe_proj = B.sem("pe_proj")
        act_ev = B.sem("act_ev")
        stg_st = [B.sem(f"stg_st{i}") for i in range(NSTG)]
        pe_f = B.sem("pe_f")
        dve_f = B.sem("dve_f")
        act_f = B.sem("act_f")
        pe_c = B.sem("pe_c")
        dve_c = B.sem("dve_c")
        pst = {"grp": 0, "ev_tk": [], "stg_n": 0, "stg_tk": [None] * NSTG, "last_mm": None}

        def proj_group(mm_list, ncols, dst_dram, split=None):
            gi = pst["grp"]; pst["grp"] += 1
            bank = P[gi % 3]
            w = []
            if gi >= 3:
                w.append(pst["ev_tk"][gi - 3])
            nmm = len(mm_list)
            tk = None
            for k, (lhsT, rhs, extra_w) in enumerate(mm_list):
                ww = list(w) if k == 0 else []
                ww += list(extra_w)
                ov = bank[:, 0:ncols] if split is None else bank[:, 0:ncols].rearrange("p (a b) -> p a b", a=split)
                fn = (lambda e, lhsT=lhsT, rhs=rhs, k=k, ov=ov: e.matmul(ov, lhsT=lhsT, rhs=rhs,
                                                                         start=(k == 0), stop=(k == nmm - 1)))
                if k == nmm - 1:
                    tk = B.op("pe", fn, waits=ww, sig=pe_proj)
                else:
                    B.op("pe", fn, waits=ww)
            si = pst["stg_n"] % NSTG; pst["stg_n"] += 1
            stg = stgK[si][:, 0:ncols]
            tk_ev = B.op("act", lambda e: e.copy(out=stg, in_=bank[:, 0:ncols]), waits=[tk, pst["stg_tk"][si]], sig=act_ev)
            pst["ev_tk"].append(tk_ev)
            pst["stg_tk"][si] = B.dma("sp", dst_dram, stg, waits=[tk_ev], sig=stg_st[si])
            return tk

        def w_in_piece(col0):
            src = w_in[:, col0:col0 + c.PC].rearrange("(c p) f -> p c f", p=128)
            return req_piece(src, (NCH, c.PC))

        C_DQ, C_DK, C_DV = 0, c.DW, 2 * c.DW
        C_FQ, C_FK, C_FV = 3 * c.DW, 3 * c.DW + c.FW, 3 * c.DW + 2 * c.FW
        Wf3 = Wf_sb.rearrange("p (c f) -> p c f", c=NCH)
        cumL3 = cumL.rearrange("p (b h) -> p b h", b=NB)

        hT_free = None
        for t in range(c.NTA):
            tk_hT = None
            for tb in range(NBA):
                blk = t * NBA + tb
                tk_hT = norm_transpose(
                    xsA, hnA, gA, x_loc[blk * 128:(blk + 1) * 128, :],
                    (lambda c0, n, tb=tb: hT4[:, c0:c0 + n, tb, :]),
                    extra_waits=[hT_free] if hT_free is not None else [], g_wait=tk_gA)
            hw = [tk_hT]
            for tb in range(NBA):
                blk = t * NBA + tb
                for cc in range(NCH):
                    fn = (lambda e, cc=cc, tb=tb: e.matmul(P[3][:, 0:NFH], lhsT=hT4[:, cc, tb, :], rhs=Wf3[:, cc, :],
                                                            start=(cc == 0), stop=(cc == NCH - 1)))
                    if cc == NCH - 1:
                        tk_f = B.op("pe", fn, sig=pe_f)
                    else:
                        w = list(hw) if cc == 0 else []
                        if cc == 0 and blk > 0:
                            w.append(pst["zf_tk"])
                        B.op("pe", fn, waits=w)
                tkz = B.op("dve", lambda e: e.tensor_tensor(out=zf, in0=P[3][:, 0:NFH], in1=bfb, op=ALU.add),
                           waits=[tk_f, pst.get("lf_free")], sig=dve_f)
                pst["zf_tk"] = tkz
                B.op("act", lambda e: e.activation(out=ef, in_=zf, func=AF.Exp, scale=-1.0), waits=[tkz, pst.get("c_tk")])
                tkl = B.op("act", lambda e: e.activation(out=lf, in_=ef, func=AF.Ln, bias=1.0, scale=1.0), sig=act_f)
                B.op("pe", lambda e: e.matmul(P[4][:, 0:NFH], lhsT=tri, rhs=lf, start=True, stop=True),
                     waits=[tkl, pst.get("cum_ev")])
                tkc = B.op("pe", lambda e: e.matmul(P[4][:, 64:64 + NFH], lhsT=ones_f, rhs=lf, start=True, stop=True),
                           sig=pe_c)
                pst["c_tk"] = tkc
                B.op("dve", lambda e, blk=blk: e.tensor_tensor(out=cumL3[:, blk, :], in0=P[4][:, 0:NFH], in1=runL,
                                                               op=ALU.add), waits=[tkc])
                tkr = B.op("dve", lambda e: e.tensor_tensor(out=runL, in0=P[4][:, 64:64 + NFH], in1=runL, op=ALU.add),
                           sig=dve_c)
                pst["lf_free"] = tkc
                tkg = B.op("pe", lambda e, blk=blk: e.matmul(P[5][0:NFH, 0:128], lhsT=cumL3[:, blk, :], rhs=ident_f,
                                                              start=True, stop=True),
                           waits=[tkr, pst.get("gq_ev")], sig=pe_c)
                tkge = B.op("dve", lambda e, blk=blk: e.tensor_scalar(out=gqT[0:NFH, blk * 64:(blk + 1) * 64],
                                                                       in0=P[5][0:NFH, 0:64], scalar1=-1.0 / c.SCALE,
                                                                       scalar2=None, op0=ALU.mult),
                            waits=[tkg], sig=dve_c)
                pst["cum_ev"] = tkr
                pst["gq_ev"] = tkge
            last_tk = None
            for (cb, hm0) in ((C_DQ, 0), (C_FQ, 2 * NDH)):
                ncolblk = (c.DW if cb == C_DQ else c.FW) // 128
                for pc0 in range(0, ncolblk * 128, c.PC):
                    pv, tkp = w_in_piece(cb + pc0)
                    for j in range(c.PC // 128):
                        hm = hm0 + (pc0 // 128) + j
                        for q0 in range(0, NBA, 8):
                            nq = min(8, NBA - q0)
                            mm = [(pv[:, cc, j * 128:(j + 1) * 128], hT4[:, cc, q0:q0 + nq, 0:64],
                                   ([tkp] + hw) if cc == 0 else []) for cc in range(NCH)]
                            o0 = t * NBA * 64 + q0 * 64
                            last_tk = proj_group(mm, nq * 64, QT_d[hm, :, o0:o0 + nq * 64], split=nq)
                    piece_consumed(last_tk)
            for (cb, hm0) in ((C_DK, 0), (C_FK, 2 * NDH)):
                ncolblk = (c.DW if cb == C_DK else c.FW) // 128
                for pc0 in range(0, ncolblk * 128, c.PC):
                    pv, tkp = w_in_piece(cb + pc0)
                    for j in range(c.PC // 128):
                        hm = hm0 + (pc0 // 128) + j
                        for k0 in range(0, NBA, 4):
                            mm = [(pv[:, cc, j * 128:(j + 1) * 128], hT4[:, cc, k0:k0 + 4, :],
                                   ([tkp] + hw) if cc == 0 else []) for cc in range(NCH)]
                            o0 = t * TA + k0 * 128
                            last_tk = proj_group(mm, 512, KT_d[hm, :, o0:o0 + 512], split=4)
                    piece_consumed(last_tk)
            for (cb, vc0, width) in ((C_DV, 0, c.DW), (C_FV, c.DW, c.FW)):
                for pc0 in range(0, width, c.PC):
                    pv, tkp = w_in_piece(cb + pc0)
                    for tb in range(NBA):
                        r0 = t * TA + tb * 128
                        mm = [(hT4[:, cc, tb, :], pv[:, cc, :], ([tkp] + hw) if cc == 0 else []) for cc in range(NCH)]
                        last_tk = proj_group(mm, c.PC, V_d[r0:r0 + 128, vc0 + pc0: vc0 + pc0 + c.PC])
                    piece_consumed(last_tk)
            hT_free = last_tk

        if debug:
            B.dma("sp", cum_d, cumL, waits=[pst["cum_ev"]], sig=setup_ld)
        B.barrier()

        cb_ = Carver()
        KTs = [cb_.take(S * 2, BF16) for _ in range(2)]
        QTs = [cb_.take(NOWN * 2, BF16) for _ in range(2)]
        VdS = [cb_.take(NB * 257 * 2 + 2, BF16) for _ in range(2)]
        vf_off = cb_.off
        VfS = [cb_.take(NB * 129 * 2 + 2, BF16) for _ in range(2)]
        assert c.NG * 4 * 256 * 4 <= cb_.off - vf_off, "O1 alias does not fit in the fox V buffers"
        O1 = big[:, vf_off // 2: vf_off // 2 + c.NG * 4 * 256 * 2].bitcast(F32)
        Wt = [cb_.take(576 * 4, F32) for _ in range(NDH + 1)]
        NPT = 4
        PTs = [cb_.take(512 * 2, BF16) for _ in range(NPT)]
        tmpS = [cb_.take(512 * 4, F32) for _ in range(3)]
        Of = cb_.take(256 * 4, F32)
        On = [cb_.take(256 * 2, BF16) for _ in range(4)]
        mixS = [cb_.take(2 * 512 * 2, BF16) for _ in range(2)]
        ohm = cb_.take((NUM_BUCKETS + 1) * 128 * 2, BF16)
        assert cb_.off <= TOP_BASE, cb_.off
        O14 = O1.rearrange("p (g a d) -> p g a d", g=c.NG, a=4)

        ohm_ld = B.sem("ohm_ld")
        tk_ohm = B.dma("pool", ohm, ohm_in, sig=ohm_ld)
        ohm3 = ohm.rearrange("p (b q) -> p b q", b=NUM_BUCKETS + 1)
        dve_w = B.sem("dve_w")
        tk_w = None
        for h in range(NDH):
            W = Wt[h]
            B.op("dve", lambda e, W=W: e.tensor_copy(out=W[:, 0:128], in_=ohm3[:, NUM_BUCKETS, :]), waits=[tk_ohm])
            for bk in range(NUM_BUCKETS):
                B.op("dve", lambda e, W=W, bk=bk, h=h: e.scalar_tensor_tensor(
                    out=W[:, 0:128], in0=ohm3[:, bk, :], scalar=tab[:, bk * NDH + h: bk * NDH + h + 1], in1=W[:, 0:128],
                    op0=ALU.mult, op1=ALU.add))
            B.op("dve", lambda e, W=W: e.memset(W[:, 128:576], 0.0))
            tk_w = B.op("dve", lambda e, W=W, h=h: e.tensor_scalar(
                out=W[:, 128:576], in0=W[:, 128:576], scalar1=tab[:, 31 * NDH + h: 31 * NDH + h + 1], scalar2=None,
                op0=ALU.add), sig=dve_w)
        W = Wt[NDH]
        B.op("dve", lambda e: e.memset(W[:, :], 0.0))
        tk_w = B.op("dve", lambda e: e.tensor_copy(out=W[:, 0:64], in_=ohm3[:, NUM_BUCKETS, 0:64]), sig=dve_w)
        for bi in range(2):
            vd3 = VdS[bi][:, 0:NB * 257].rearrange("p (b d) -> p b d", b=NB)
            vf3 = VfS[bi][:, 0:NB * 129].rearrange("p (b d) -> p b d", b=NB)
            tk_w = B.op("dve", lambda e, vd3=vd3: e.memset(vd3[:, :, 256:257], 1.0))

        kq_ld = [B.sem("kq_ld0"), B.sem("kq_ld1")]
        vd_ld = [B.sem("vd_ld0"), B.sem("vd_ld1")]
        vf_ld = [B.sem("vf_ld0"), B.sem("vf_ld1")]
        pe_s = B.sem("pe_s")
        dve_pre = B.sem("dve_pre")
        act_exp = B.sem("act_exp")
        pe_pv = B.sem("pe_pv")
        pe_o = B.sem("pe_o")
        dve_o = B.sem("dve_o")
        pe_ot = B.sem("pe_ot")
        act_ot = B.sem("act_ot")
        mix_st = [B.sem("mix_st0"), B.sem("mix_st1")]

        V_d3 = V_d.rearrange("(b p) v -> p b v", p=128)
        ast = {"n": 0, "s_tk": {}, "exp_tk": {}, "pv_tk": {}, "unit": 0, "kq_free": [None, None],
               "vd_n": 0, "vf_n": 0, "vd_free": [None, None], "vf_free": [None, None], "o_ev": None,
               "on_n": 0, "ot_ev": [None, None, None, None], "ot_n": 0, "mix_n": 0, "mix_free": [None, None], "pre_tk": {}}

        units = []
        for h in range(NDH):
            units.append(("d", h, 0)); units.append(("d", h, 1))
        for h in range(NFH):
            units.append(("f", h, 0))

        def load_unit(ui):
            kind, h, m = units[ui]
            b = ui % 2
            hm = (2 * h + m) if kind == "d" else (2 * NDH + h)
            w = [ast["kq_free"][b]]
            B.dma("sp", KTs[b], KT_d[hm], waits=w, sig=kq_ld[b])
            tk = B.dma("sp", QTs[b], QT_d[hm], sig=kq_ld[b])
            tkv = None
            if kind == "d" and m == 0:
                vb = ast["vd_n"] % 2; ast["vd_n"] += 1
                vd3 = VdS[vb][:, 0:NB * 257].rearrange("p (b d) -> p b d", b=NB)
                tkv = B.dma("sp", vd3[:, :, 0:256], V_d3[:, :, h * 256:(h + 1) * 256],
                            waits=[ast["vd_free"][vb], tk_w], sig=vd_ld[vb])
                ast["cur_vd"] = (vb, tkv)
            elif kind == "f":
                if ast.get("diff_final") is None:
                    ast.setdefault("deferred", []).append(ui)
                else:
                    load_vf(ui)
            return tk

        def load_vf(ui):
            kind, h, m = units[ui]
            vb = ast["vf_n"] % 2; ast["vf_n"] += 1
            vf3 = VfS[vb][:, 0:NB * 129].rearrange("p (b d) -> p b d", b=NB)
            if ast["vf_n"] <= 2:
                tko = B.op("dve", lambda e, vf3=vf3: e.memset(vf3[:, :, 128:129], 1.0), waits=[ast["diff_final"]])
            else:
                tko = None
            tkv = B.dma("sp", vf3[:, :, 0:128], V_d3[:, :, c.DW + h * 128: c.DW + (h + 1) * 128],
                        waits=[ast["vf_free"][vb], tk_w, ast["diff_final"], tko], sig=vf_ld[vb])
            unit_v[ui] = (vb, tkv, tko)

        unit_ld = {}
        unit_v = {}

        def issue_load(ui):
            if ui < len(units) and ui not in unit_ld:
                unit_ld[ui] = load_unit(ui)
                kind, h, m = units[ui]
                if kind == "d":
                    unit_v[ui] = ast["cur_vd"] + (None,)

        issue_load(0)
        issue_load(1)

        for ui, (kind, h, m) in enumerate(units):
            b = ui % 2
            tk_kq = unit_ld[ui]
            vb, tk_v, tk_ones = unit_v[ui]
            O13 = O14[:, 0, :, :]
            KT = KTs[b]; QT = QTs[b]
            if kind == "d":
                V3 = VdS[vb][:, 0:NB * 257].rearrange("p (b d) -> p b d", b=NB)
                NV = 257
                Wm = Wt[h]
            else:
                V3 = VfS[vb][:, 0:NB * 129].rearrange("p (b d) -> p b d", b=NB)
                NV = 129
                Wm = Wt[NDH]
            last_pe = None
            for g in range(c.NG):
                O13 = O14[:, g, :, :]
                nj = 8 * g + 8
                tiles = []
                for j in range(nj):
                    mrel = j - 8 * g
                    lo = 64 * mrel if mrel >= 0 else 0
                    n = 512 - lo
                    if kind == "d":
                        special = (mrel >= -1)
                        woff = 64 if mrel == -1 else 0
                    else:
                        special = (mrel >= 0)
                        woff = 0
                    tiles.append((j, lo, n, special, woff))

                def emit_S(idx):
                    j, lo, n, special, woff = tiles[idx]
                    k = ast["n"] + idx
                    sbk = SB[k % NSB]
                    w = [tk_kq, tk_setup_dve, ast.get(("ptev", 0))]
                    if k >= NSB:
                        w.append(ast["exp_tk"].get(k - NSB))
                        w.append(ast["pre_tk"].get(k - NSB))
                    lhsT = KT[:, j * 128:(j + 1) * 128]
                    rhs = QT[:, g * 512 + lo: (g + 1) * 512]
                    if kind == "d":
                        ast["s_tk"][k] = B.op("pe", lambda e: e.matmul(sbk[:, 0:n], lhsT=lhsT, rhs=rhs, start=True, stop=True),
                                              waits=w, sig=pe_s)
                    else:
                        B.op("pe", lambda e: e.matmul(sbk[:, 0:n], lhsT=lhsT, rhs=rhs, start=True, stop=False), waits=w)
                        lhs2 = sel[:, h * 128:(h + 1) * 128]
                        rhs2 = gqT[:, g * 512 + lo:(g + 1) * 512]
                        ast["s_tk"][k] = B.op("pe", lambda e: e.matmul(sbk[:, 0:n], lhsT=lhs2, rhs=rhs2, start=False,
                                                                        stop=True), sig=pe_s)

                def emit_exp(idx):
                    j, lo, n, special, woff = tiles[idx]
                    k = ast["n"] + idx
                    sbk = SB[k % NSB]
                    pt = PTs[k % NPT]
                    wpt = ast["pv_tk"].get(k - NPT)
                    if kind == "d":
                        bias_full = tab[:, 31 * NDH + h: 31 * NDH + h + 1]
                        bias_sp = zero_c
                    else:
                        bias_full = cumL3[:, j, h:h + 1]
                        bias_sp = bias_full
                    if special:
                        if kind == "d":
                            wc = min(n, 128) if woff == 0 else 64
                        else:
                            wc = 64
                        wc = n
                        tm = tmpS[k % 3]
                        wv = Wm[:, woff:woff + wc]
                        tkp = B.op("dve", lambda e: e.scalar_tensor_tensor(
                            out=tm[:, 0:wc], in0=sbk[:, 0:wc], scalar=c.SCALE, in1=wv, op0=ALU.mult,
                            op1=ALU.add), waits=[ast["s_tk"][k], ast["exp_tk"].get(k - 3), tk_w], sig=dve_pre, indep=True)
                        ast["pre_tk"][k] = tkp
                        if n > wc:
                            B.op("act", lambda e: e.activation(out=pt[:, wc:n], in_=sbk[:, wc:n], func=AF.Exp,
                                                               bias=bias_full, scale=c.SCALE),
                                 waits=[ast["s_tk"][k], wpt], sig=act_exp, indep=True)
                        ast["exp_tk"][k] = B.op("act", lambda e: e.activation(out=pt[:, 0:wc], in_=tm[:, 0:wc], func=AF.Exp,
                                                                               bias=bias_sp, scale=1.0),
                                                waits=[tkp, wpt], sig=act_exp, indep=True)
                    else:
                        ast["exp_tk"][k] = B.op("act", lambda e: e.activation(out=pt[:, 0:n], in_=sbk[:, 0:n], func=AF.Exp,
                                                                               bias=bias_full, scale=c.SCALE),
                                                waits=[ast["s_tk"][k], wpt], sig=act_exp, indep=True)

                def emit_pv(idx):
                    j, lo, n, special, woff = tiles[idx]
                    k = ast["n"] + idx
                    pt = PTs[k % NPT]
                    tk = None
                    first = True
                    for qc in range(4):
                        c_lo = max(128 * qc, lo)
                        c_hi = 128 * qc + 128
                        if c_lo >= c_hi:
                            continue
                        r0 = c_lo - 128 * qc
                        i1 = 8 * g + 2 * qc + 1
                        w = [ast["exp_tk"][k], tk_v, tk_ones] if first else []
                        if j == 0:
                            w.append(ast["o_ev"])
                        first = False
                        rv_ = V3[:, j, 0:NV]
                        if j == i1 - 1 and r0 == 0 and not os.environ.get('NOSPLITMM'):
                            ov = P[2 + qc][0:64, 0:NV]
                            lv_ = pt[:, c_lo - lo:c_lo - lo + 64]
                            B.op("pe", lambda e, ov=ov, lv_=lv_, rv_=rv_, st_=(j == 0): e.matmul(
                                ov, lhsT=lv_, rhs=rv_, start=st_, stop=True), waits=w)
                            w = []
                            ov = P[2 + qc][64:128, 0:NV]
                            lv_ = pt[:, c_lo - lo + 64:c_hi - lo]
                            tk = B.op("pe", lambda e, ov=ov, lv_=lv_, rv_=rv_, st_=(j == 0): e.matmul(
                                ov, lhsT=lv_, rhs=rv_, start=st_, stop=False), waits=w, sig=pe_pv)
                        else:
                            ov = P[2 + qc][r0:128, 0:NV]
                            lv_ = pt[:, c_lo - lo:c_hi - lo]
                            tk = B.op("pe", lambda e, ov=ov, lv_=lv_, rv_=rv_, st_=(j == 0), sp_=(j == i1): e.matmul(
                                ov, lhsT=lv_, rhs=rv_, start=st_, stop=sp_), waits=w, sig=pe_pv)
                    ast["pv_tk"][k] = tk
                    return tk

                nt = len(tiles)
                for i_ in range(min(NSB, nt)):
                    emit_S(i_)
                p2a_at = 1
                p2b_at = min(nt - 1, 12)
                for idx in range(nt):
                    emit_exp(idx)
                    if idx + NSB < nt:
                        emit_S(idx + NSB)
                    last_pe = emit_pv(idx)
                    if idx == p2a_at and ast.get("pend_a"):
                        ast["pend_a"](); ast["pend_a"] = None
                    if idx == p2b_at and ast.get("pend_b"):
                        ast["pend_b"](); ast["pend_b"] = None
                ast["n"] += nt
                tk_o = last_pe

                if kind == "d" and m == 0:
                    tkr = []
                    for qc in range(4):
                        tkr.append(B.op("dve", lambda e, qc=qc: e.reciprocal(out=rlb[:, qc:qc + 1], in_=P[2 + qc][:, 256:257]),
                                        waits=[tk_o, ast.get("o1_free")], sig=dve_o, indep=(qc > 0)))
                    for qc in range(4):
                        tk_e = B.op("dve", lambda e, qc=qc, O13=O13: e.tensor_scalar(out=O13[:, qc, :], in0=P[2 + qc][:, 0:256],
                                                                            scalar1=rlb[:, qc:qc + 1], scalar2=None,
                                                                            op0=ALU.mult), waits=[tkr[qc]], sig=dve_o, indep=True)
                    ast["o_ev"] = tk_e
                else:
                    mi = ast["mix_n"] % 2; ast["mix_n"] += 1
                    nchk = 2 if kind == "d" else 1
                    ms3 = mixS[mi].rearrange("p (a q) -> p a q", a=2)
                    ons = []
                    for qc in range(4):
                        oi = ast["on_n"] % 4; ast["on_n"] += 1
                        ons.append(oi)
                    if kind == "d":
                        tkr = []
                        tkn = []
                        for qc in range(4):
                            tkr.append(B.op("dve", lambda e, qc=qc: e.reciprocal(out=rlb[:, qc:qc + 1], in_=P[2 + qc][:, 256:257]),
                                            waits=[tk_o], sig=dve_o, indep=(qc > 0)))
                        for qc in range(4):
                            tkn.append(B.op("dve", lambda e, qc=qc: e.tensor_scalar(out=nlb[:, qc:qc + 1], in0=rlb[:, qc:qc + 1],
                                                                                    scalar1=neglam, scalar2=None, op0=ALU.mult),
                                            waits=[tkr[qc]], sig=dve_o, indep=True))
                        for qc in range(4):
                            tk_e = B.op("dve", lambda e, qc=qc, O13=O13: e.scalar_tensor_tensor(
                                out=O13[:, qc, :], in0=P[2 + qc][:, 0:256], scalar=nlb[:, qc:qc + 1], in1=O13[:, qc, :],
                                op0=ALU.mult, op1=ALU.add), waits=[tkn[qc]], sig=dve_o, indep=True)
                        tk_on_list = [None] * 4
                    else:
                        tk_on_list = []
                        for qc in range(4):
                            on = On[ons[qc]]
                            B.op("dve", lambda e, qc=qc: e.reciprocal(out=rlb[:, qc:qc + 1], in_=P[2 + qc][:, 128:129]),
                                 waits=[tk_o])
                            tk_e = B.op("dve", lambda e, qc=qc, on=on: e.tensor_scalar(
                                out=on[:, 0:128], in0=P[2 + qc][:, 0:128], scalar1=rlb[:, qc:qc + 1], scalar2=None,
                                op0=ALU.mult), waits=[ast["ot_ev"][ons[qc]]], sig=dve_o)
                            tk_on_list.append(tk_e)
                    ast["o_ev"] = tk_e
                    if kind == "d":
                        ch0 = 2 * h
                    else:
                        ch0 = 2 * NDH + h
                    dstm = mixT_d[ch0:ch0 + nchk, :, g * 512:(g + 1) * 512].rearrange("a p q -> p a q")

                    def phase2a(kind=kind, O13=O13, ons=ons, tk_on_list=tk_on_list):
                        if kind != "d":
                            return
                        for qc in range(4):
                            on = On[ons[qc]]
                            tk_s = B.op("dve", lambda e, qc=qc, O13=O13: e.scalar_tensor_tensor(
                                out=Of, in0=O13[:, qc, :], scalar=1.0, in1=O13[:, qc, :], op0=ALU.mult, op1=ALU.mult,
                                accum_out=ssb[:, 2:3]), sig=dve_o)
                            tk_rs = rstd_act(ssb[:, 2:3], rsb[:, 2:3], 1.0 / 256, tk_s)
                            tk_on_list[qc] = B.op("dve", lambda e, qc=qc, on=on, O13=O13: e.scalar_tensor_tensor(
                                out=on[:, 0:256], in0=O13[:, qc, :], scalar=rsb[:, 2:3], in1=gsub, op0=ALU.mult,
                                op1=ALU.mult), waits=[ast["ot_ev"][ons[qc]], tk_rs], sig=dve_o)
                        ast["o1_free"] = tk_on_list[3]

                    def phase2b(nchk=nchk, ons=ons, tk_on_list=tk_on_list, ms3=ms3, mi=mi, dstm=dstm):
                        tk_last_ev = None
                        for qc in range(4):
                            on = On[ons[qc]]
                            ti = 0
                            pb = PT[ti]
                            w = [tk_on_list[qc], ast.get(("ptev", ti))]
                            tkt = None
                            for a in range(nchk):
                                tkt = B.op("pe", lambda e, a=a, on=on, pb=pb: e.transpose(
                                    out=pb[:, a * 128:(a + 1) * 128], in_=on[:, a * 128:(a + 1) * 128], identity=ident[:]),
                                    waits=w if a == 0 else (), sig=pe_ot)
                            ast["ot_ev"][ons[qc]] = tkt
                            src = pb[:, 0:nchk * 128].rearrange("p (a q) -> p a q", a=nchk)
                            dstv = ms3[:, 0:nchk, qc * 128:(qc + 1) * 128]
                            tk_last_ev = B.op("act", lambda e, src=src, dstv=dstv: e.copy(out=dstv, in_=src),
                                              waits=[tkt, ast["mix_free"][mi]], sig=act_ot)
                            ast[("ptev", ti)] = tk_last_ev
                        ast["mix_free"][mi] = B.dma("sp", dstm, ms3[:, 0:nchk, :], waits=[tk_last_ev], sig=mix_st[mi])

                    if ast.get("pend_a"):
                        ast["pend_a"](); ast["pend_a"] = None
                    if ast.get("pend_b"):
                        ast["pend_b"](); ast["pend_b"] = None
                    ast["pend_a"] = phase2a
                    ast["pend_b"] = phase2b
                    if os.environ.get("NODEFER"):
                        ast["pend_a"](); ast["pend_a"] = None
                        ast["pend_b"](); ast["pend_b"] = None
                    if kind == "d" and ui == 2 * NDH - 1 and g == c.NG - 1:
                        if ast.get("pend_a"):
                            ast["pend_a"](); ast["pend_a"] = None
                        if ast.get("pend_b"):
                            ast["pend_b"](); ast["pend_b"] = None
            ast["kq_free"][b] = last_pe
            if kind == "d" and m == 1:
                ast["vd_free"][vb] = last_pe
            if kind == "f":
                ast["vf_free"][vb] = last_pe
            issue_load(ui + 2)
            if ui == 2 * NDH - 1:
                ast["diff_final"] = ast["o1_free"]
                for du in ast.get("deferred", []):
                    load_vf(du)

        if ast.get("pend_a"):
            ast["pend_a"](); ast["pend_a"] = None
        if ast.get("pend_b"):
            ast["pend_b"](); ast["pend_b"] = None
        B.barrier()

        cc_ = Carver()
        actT = cc_.take(NFC * 512 * 2, BF16)
        U = cc_.take(NCH * 512 * 2, BF16)
        NXS = 3
        xst = [cc_.take(512 * 4, F32) for _ in range(NXS)]
        ost = [cc_.take(512 * 4, F32) for _ in range(NXS)]
        sgs = [cc_.take(512 * 4, F32) for _ in range(2)]
        junkF = cc_.take(512 * 4, F32)
        assert cc_.off <= BIG_E * 2, cc_.off
        actT3 = actT.rearrange("p (f t) -> p f t", f=NFC)
        mixT = big[:, 0:NCH * 512]
        mixT3 = mixT.rearrange("p (c t) -> p c t", c=NCH)
        o2 = NCH * 512
        xsC0 = big[:, o2: o2 + 2 * D].bitcast(F32)
        xsC = [xsC0, xsC0]
        o3 = o2 + 2 * D
        hnC = [big[:, o3 + i * D: o3 + (i + 1) * D] for i in range(2)]
        o4 = o3 + 2 * D
        gF = big[:, o4: o4 + 2 * D].bitcast(F32)
        assert o4 + 2 * D <= NFC * 512, "alias overflow"
        h2T = U
        h2T4 = h2T.rearrange("p (c t q) -> p c t q", c=NCH, t=4)
        Ub = NFC * 512
        gFin = big[:, Ub: Ub + 2 * D].bitcast(F32)
        xrow = big[:, Ub + 2 * D: Ub + 4 * D].bitcast(F32)
        assert 4 * D <= NCH * 512

        gF_ld = B.sem("gF_ld")
        mix_ld = B.sem("mix_ld")
        xst_ld = [B.sem(f"xst_ld{i}") for i in range(NXS)]
        ost_st = [B.sem(f"ost_st{i}") for i in range(NXS)]
        pe_acc = B.sem("pe_acc")
        dve_ev = B.sem("dve_ev")
        pe_gu = B.sem("pe_gu")
        act_silu = B.sem("act_silu")
        dve_act = B.sem("dve_act")
        gfin_ld = B.sem("gfin_ld")
        xrow_ld = B.sem("xrow_ld")
        xrow_st = B.sem("xrow_st")
        dve_fin = B.sem("dve_fin")
        cst = {"xs_n": 0, "xst_free": [None] * NXS, "ost_free": [None] * NXS, "acc_ev": [None] * 4, "gu_n": 0,
               "gu_ev": {}, "sg_free": [None, None], "fin_tk": None, "xrow_free": None}

        def acc_phase(t, wsrc_fn, nrows_chunks, lhs_fn, lhs_waits, res_dram, dst_dram, after_n=None):
            last = None
            for n in range(D // 512):
                pieces = []
                for k0 in range(0, nrows_chunks, c.KC):
                    pieces.append((k0, min(c.KC, nrows_chunks - k0)))
                tks = [None] * 4
                for pi, (k0, kc) in enumerate(pieces):
                    pv, tkp = req_piece(wsrc_fn(k0, kc, n), (kc, 512))
                    for tb in range(4):
                        for k in range(kc):
                            ch = k0 + k
                            first = (ch == 0); lastk = (ch == nrows_chunks - 1)
                            w = []
                            if k == 0:
                                w = [tkp] + list(lhs_waits)
                                if first:
                                    w.append(cst["acc_ev"][tb])
                            fn = (lambda e, tb=tb, ch=ch, k=k, pv=pv, first=first, lastk=lastk: e.matmul(
                                P[tb][:, :], lhsT=lhs_fn(ch, tb), rhs=pv[:, k, :], start=first, stop=lastk))
                            if lastk or (tb == 3 and k == kc - 1):
                                tk = B.op("pe", fn, waits=w, sig=pe_acc)
                                if lastk:
                                    tks[tb] = tk
                                last = tk
                            else:
                                B.op("pe", fn, waits=w)
                    piece_consumed(last)
                for tb in range(4):
                    r0 = t * 512 + tb * 128
                    xi = cst["xs_n"] % NXS; cst["xs_n"] += 1
                    tk_x = B.dma("sp", xst[xi], res_dram[r0:r0 + 128, n * 512:(n + 1) * 512],
                                 waits=[cst["xst_free"][xi]], sig=xst_ld[xi])
                    tk_e = B.op("dve", lambda e, tb=tb, xi=xi: e.tensor_tensor(out=ost[xi], in0=P[tb][:, :], in1=xst[xi],
                                                                               op=ALU.add),
                                waits=[tks[tb], tk_x, cst["ost_free"][xi]], sig=dve_ev)
                    cst["acc_ev"][tb] = tk_e
                    cst["xst_free"][xi] = tk_e
                    cst["ost_free"][xi] = B.dma("sp", dst_dram[r0:r0 + 128, n * 512:(n + 1) * 512], ost[xi], waits=[tk_e],
                                                sig=ost_st[xi])
                if after_n is not None:
                    after_n(n)
            return last

        pending_final = []

        def run_pending_final(n=None):
            if pending_final:
                pending_final.pop(0)()

        prev_down_last = None
        for t in range(c.NTC):
            B.dma("sp", mixT3, mixT_d[:, :, t * 512:(t + 1) * 512].rearrange("c p q -> p c q"), waits=[prev_down_last],
                  sig=mix_ld)
            tk_mix = (mix_ld, mix_ld.n)
            tk_gF = B.dma("sp", gF, g_ffn, waits=[prev_down_last], sig=gF_ld)
            last_wo = acc_phase(
                t, (lambda k0, kc, n: w_o[k0 * 128:(k0 + kc) * 128, n * 512:(n + 1) * 512].rearrange("(k p) f -> p k f", p=128)),
                NCH, (lambda ch, tb: mixT3[:, ch, tb * 128:(tb + 1) * 128]), [tk_mix], x_own, x1_d,
                after_n=run_pending_final)
            while pending_final:
                run_pending_final()
            x1_done = [cst["ost_free"][i] for i in range(NXS)]
            tk_h2 = None
            for tb in range(4):
                r0 = t * 512 + tb * 128
                tk_h2 = norm_transpose(xsC, hnC, gF, x1_d[r0:r0 + 128, :],
                                       (lambda c0, n, tb=tb: h2T4[:, c0:c0 + n, tb, :]),
                                       extra_waits=x1_done + [cst["fin_tk"], last_wo], g_wait=tk_gF)
            last_gu = None
            for f0 in range(0, c.DFF, c.PC):
                pg, tkg = req_piece(w_gate[:, f0:f0 + c.PC].rearrange("(c p) f -> p c f", p=128), (NCH, c.PC))
                pu, tku = req_piece(w_up[:, f0:f0 + c.PC].rearrange("(c p) f -> p c f", p=128), (NCH, c.PC))
                lastm = None
                for j in range(c.PC // 128):
                    fi = f0 // 128 + j
                    gi = cst["gu_n"]; cst["gu_n"] += 1
                    bg = P[(gi % 2) * 2]; bu = P[(gi % 2) * 2 + 1]
                    w0 = [tkg, tku, tk_h2, cst["gu_ev"].get(gi - 2)]
                    for (pv, bank, issig) in ((pg, bg, False), (pu, bu, True)):
                        for ch in range(NCH):
                            fn = (lambda e, pv=pv, bank=bank, ch=ch, j=j: e.matmul(
                                bank[:, :], lhsT=pv[:, ch, j * 128:(j + 1) * 128], rhs=h2T[:, ch * 512:(ch + 1) * 512],
                                start=(ch == 0), stop=(ch == NCH - 1)))
                            if issig and ch == NCH - 1:
                                lastm = B.op("pe", fn, sig=pe_gu)
                            else:
                                B.op("pe", fn, waits=w0 if ch == 0 else ())
                    si = gi % 2
                    tk_s = B.op("act", lambda e, bg=bg, si=si: e.activation(out=sgs[si], in_=bg[:, :], func=AF.Silu),
                                waits=[lastm, cst["sg_free"][si]], sig=act_silu)
                    tk_a = B.op("dve", lambda e, bu=bu, si=si, fi=fi: e.tensor_tensor(out=actT3[:, fi, :], in0=sgs[si],
                                                                                      in1=bu[:, :], op=ALU.mult),
                                waits=[tk_s], sig=dve_act)
                    cst["sg_free"][si] = tk_a
                    cst["gu_ev"][gi] = tk_a
                piece_consumed(lastm)
                piece_consumed(lastm)
                last_gu = lastm
            tk_act_done = cst["gu_ev"][cst["gu_n"] - 1]
            last_dn = acc_phase(
                t, (lambda k0, kc, n: w_down[k0 * 128:(k0 + kc) * 128, n * 512:(n + 1) * 512].rearrange("(k p) f -> p k f", p=128)),
                NFC, (lambda ch, tb: actT3[:, ch, tb * 128:(tb + 1) * 128]), [tk_act_done], x1_d, out_d)
            prev_down_last = last_dn
            x2_done = [cst["ost_free"][i] for i in range(NXS)]
            def final_block(tb, t=t, x2_done=x2_done, last_gu=last_gu):
                if tb == 0:
                    cst["gfin_tk"] = B.dma("act", gFin, g_fin, waits=[last_gu], sig=gfin_ld)
                tk_gfin = cst["gfin_tk"]
                r0 = t * 512 + tb * 128
                tk_r = B.dma("act", xrow, out_d[r0:r0 + 128, :], waits=x2_done + [cst["xrow_free"], last_gu], sig=xrow_ld)
                B.op("dve", lambda e: e.memset(ssb[:, 3:4], 0.0), waits=[tk_r])
                for q8 in range(D // 512):
                    B.op("dve", lambda e, q8=q8: e.scalar_tensor_tensor(
                        out=junkF, in0=xrow[:, q8 * 512:(q8 + 1) * 512], scalar=1.0, in1=xrow[:, q8 * 512:(q8 + 1) * 512],
                        op0=ALU.mult, op1=ALU.mult, accum_out=rlb[:, q8 % 8:q8 % 8 + 1]))
                    B.op("dve", lambda e, q8=q8: e.tensor_tensor(out=ssb[:, 3:4], in0=ssb[:, 3:4],
                                                                 in1=rlb[:, q8 % 8:q8 % 8 + 1], op=ALU.add))
                tk_ss = B.op("dve", lambda e: e.tensor_copy(out=ssb[:, 3:4], in_=ssb[:, 3:4]), sig=dve_ss)
                tk_rs = rstd_act(ssb[:, 3:4], rsb[:, 3:4], 1.0 / D, tk_ss)
                tk_fn = B.op("dve", lambda e: e.scalar_tensor_tensor(out=xrow, in0=xrow, scalar=rsb[:, 3:4], in1=gFin,
                                                                     op0=ALU.mult, op1=ALU.mult),
                             waits=[tk_gfin, tk_rs], sig=dve_fin)
                cst["xrow_free"] = B.dma("act", out_d[r0:r0 + 128, :], xrow, waits=[tk_fn], sig=xrow_st)
                cst["fin_tk"] = cst["xrow_free"]

            for tb in range(4):
                pending_final.append(lambda tb=tb, fb=final_block: fb(tb))

        while pending_final:
            run_pending_final()

        B.barrier()
        B.finish()
    return nc


def _bucket(n):
    max_exact = NUM_BUCKETS // 2
    nf = np.maximum(n, 1).astype(np.float32)
    large = max_exact + (np.log(nf / np.float32(max_exact)) / np.float32(math.log(MAX_DISTANCE / max_exact))
                         * np.float32(NUM_BUCKETS - max_exact)).astype(np.int32)
    large = np.minimum(large, NUM_BUCKETS - 1)
    return np.where(n < max_exact, n, large)


def _bucket_jax(n):
    import jax
    import jax.numpy as jnp
    with jax.default_device(jax.devices("cpu")[0]):
        max_exact = NUM_BUCKETS // 2
        nn = jnp.asarray(n, dtype=jnp.int32)
        nf = jnp.maximum(nn, 1).astype(jnp.float32)
        large = max_exact + (jnp.log(nf / max_exact) / math.log(MAX_DISTANCE / max_exact)
                             * (NUM_BUCKETS - max_exact)).astype(jnp.int32)
        large = jnp.minimum(large, NUM_BUCKETS - 1)
        return np.asarray(jnp.where(nn < max_exact, nn, large))


def core_consts(r):
    p = np.arange(128)
    glob = np.where(p < 64, 64 * r + p, 64 * (1 - r) + (p - 64))
    tri = (glob[:, None] <= glob[None, :]).astype(np.float32)
    ohm = np.zeros((128, NUM_BUCKETS + 1, 128), np.float32)
    qq = np.arange(64)
    try:
        bfun = _bucket_jax
        bfun(np.arange(4))
    except Exception:
        bfun = _bucket
    for u in range(2):
        dist = 128 * u + (64 * r + qq)[None, :] - glob[:, None]
        bk = bfun(np.maximum(dist, 0))
        for b in range(NUM_BUCKETS):
            ohm[:, b, u * 64:(u + 1) * 64] = ((bk == b) & (dist >= 0)).astype(np.float32)
        ohm[:, NUM_BUCKETS, u * 64:(u + 1) * 64] = np.where(dist >= 0, 0.0, NEG)
    return tri, ohm.reshape(128, -1)


def make_in_maps(cfg, inputs):
    c = cfg
    x = np.asarray(inputs["x"], np.float32)
    nb = x.shape[0]
    f = lambda k: np.ascontiguousarray(np.asarray(inputs[k], np.float32))
    w_in = f("w_in")[0]; w_o = f("w_o")[0]; w_gate = f("w_gate")[0]; w_up = f("w_up")[0]; w_down = f("w_down")[0]
    bc = lambda v: np.ascontiguousarray(np.broadcast_to(np.asarray(v, np.float32).reshape(1, -1), (128, v.size)))
    g_attn = bc(f("attn_norm_g")[0]); g_ffn = bc(f("ffn_norm_g")[0]); g_fin = bc(f("final_norm_g"))
    g_sub = bc(f("diff_subln_g")[0]); bf_bc = bc(f("b_f")[0])
    lam = np.concatenate([f("lambda_q1")[0], f("lambda_k1")[0], f("lambda_q2")[0], f("lambda_k2")[0]])
    lam_bc = bc(lam)
    tab_bc = bc(f("rel_bias_table").reshape(-1))
    sel = np.zeros((128, c.NFH, 128), np.float32)
    for h in range(c.NFH):
        sel[h, h, :] = 1.0
    sel = sel.reshape(128, -1)
    ident = np.eye(128, dtype=np.float32)
    consts = [core_consts(0), core_consts(1)]
    in_maps = []
    perms = []
    for core in range(2 * nb):
        b, r = core // 2, core % 2
        xb = x[b].reshape(c.NB, 2, 64, c.D)
        x_loc = np.ascontiguousarray(np.stack([xb[:, r], xb[:, 1 - r]], axis=1).reshape(c.S, c.D))
        x_own = np.ascontiguousarray(xb[:, r].reshape(c.NOWN, c.D))
        tri, ohm = consts[r]
        in_maps.append(dict(x_loc=x_loc, x_own=x_own, w_in=w_in, w_o=w_o, w_gate=w_gate, w_up=w_up, w_down=w_down,
                            g_attn=g_attn, g_ffn=g_ffn, g_fin=g_fin, g_sub=g_sub, bf_bc=bf_bc, lam_bc=lam_bc,
                            tab_bc=tab_bc, tri=tri, ohm=ohm, sel=sel, ident=ident))
    return in_maps


def assemble(cfg, results, nb):
    c = cfg
    out = np.empty((nb, c.NB, 2, 64, c.D), np.float32)
    for core in range(2 * nb):
        b, r = core // 2, core % 2
        out[b, :, r] = np.asarray(results[core]["out"]).reshape(c.NB, 64, c.D)
    return out.reshape(nb, c.S, c.D)


_CACHE = {}


def kernel(**inputs):
    cfg = Cfg()
    if "nc" not in _CACHE:
        _CACHE["nc"] = build(cfg)
    nc = _CACHE["nc"]
    in_maps = make_in_maps(cfg, inputs)
    nb = np.asarray(inputs["x"]).shape[0]
    res = run_bass_kernel_spmd(nc, in_maps, core_ids=list(range(2 * nb)))
    return assemble(cfg, res.results, nb)
```

```python
import math
import os
from contextlib import ExitStack

import numpy as np
import concourse.bass as bass
import concourse.mybir as mybir
from concourse.bass_utils import run_bass_kernel_spmd

F32 = mybir.dt.float32
BF16 = mybir.dt.bfloat16
AF = mybir.ActivationFunctionType
ALU = mybir.AluOpType
AX = mybir.AxisListType

EPS = 1e-6
NUM_BUCKETS = 32
MAX_DISTANCE = 128
NEG = -30000.0
LAM_INIT = 0.8 - 0.6 * math.exp(-0.3 * 0)


class Cfg:
    def __init__(s, D=4096, S=4096, DFF=11008):
        s.D = D; s.S = S; s.DFF = DFF
        s.NDH = D // 512; s.NFH = D // 256
        s.DW = s.NDH * 256; s.FW = s.NFH * 128
        s.INC = 3 * s.DW + 3 * s.FW + s.NFH
        s.NCH = D // 128; s.NB = S // 128; s.NOWN = S // 2
        s.NFC = DFF // 128
        s.NHM = 2 * s.NDH + s.NFH
        s.NBA = 8
        s.NTA = S // (128 * s.NBA)
        s.NTC = s.NOWN // 512
        s.NG = s.NOWN // 512
        s.PC = 256
        s.KC = min(16, s.NCH)
        s.SCALE = 128 ** -0.5


class Sem:
    pass


class Builder:
    ENG = ("pe", "act", "dve", "pool", "sp")

    def __init__(self, nc, es):
        self.nc = nc; self.es = es
        self.q = {e: [] for e in self.ENG}
        self.sems = []
        self.waited = {e: {} for e in self.ENG}
        self.seq = {}

    def sem(self, name):
        s = Sem(); s.h = self.es.enter_context(self.nc.semaphore(name)); s.n = 0; s.name = name
        self.sems.append(s)
        return s

    def _ws(self, eng, waits):
        ws = []
        wd = self.waited[eng]
        for w in waits:
            if w is None:
                continue
            s, t = w
            if t is None or t <= 0:
                continue
            if wd.get(s.name, 0) >= t:
                continue
            wd[s.name] = t
            ws.append((s.h, t))
        return ws

    def op(self, eng, fn, waits=(), sig=None, step=1, indep=False, is_dma=False):
        if eng in ("act", "dve") and not is_dma:
            if eng not in self.seq:
                self.seq[eng] = self.sem("seq_" + eng)
            sq = self.seq[eng]
            waits = list(waits)
            if sq.n > 0 and not (indep and not os.environ.get('NOINDEP')):
                waits.append((sq, sq.n))
            sig = sq
            step = 1
        ws = self._ws(eng, waits)
        ticket = None
        sh = None
        if sig is not None:
            sig.n += step; ticket = sig.n; sh = sig.h

        def run(e):
            for h, t in ws:
                e.wait_ge(h, t)
            ins = fn(e)
            if sh is not None:
                ins.then_inc(sh, step)
        self.q[eng].append(run)
        return (sig, ticket) if sig is not None else None

    def dma(self, eng, out, in_, waits=(), sig=None):
        return self.op(eng, lambda e: e.dma_start(out=out, in_=in_), waits, sig, step=16, is_dma=True)

    def wait(self, eng, waits):
        ws = self._ws(eng, waits)
        if ws:
            def run(e):
                for h, t in ws:
                    e.wait_ge(h, t)
            self.q[eng].append(run)

    def barrier(self):
        for eng in self.ENG:
            self.wait(eng, [(s, s.n) for s in self.sems if s.n > 0])

    def finish(self):
        nc = self.nc
        with nc.Block() as block:
            q = self.q

            @block.tensor
            def _(e):
                for f in q["pe"]:
                    f(e)

            @block.scalar
            def _(e):
                for f in q["act"]:
                    f(e)

            @block.vector
            def _(e):
                for f in q["dve"]:
                    f(e)

            @block.gpsimd
            def _(e):
                for f in q["pool"]:
                    f(e)

            @block.sync
            def _(e):
                for f in q["sp"]:
                    f(e)


def build(cfg, debug=False):
    c = cfg
    D, S, NCH, NB, NOWN = c.D, c.S, c.NCH, c.NB, c.NOWN
    NDH, NFH, NHM, NFC = c.NDH, c.NFH, c.NHM, c.NFC
    nc = bass.Bass("TRN2", target_bir_lowering=False)

    def din(name, shape, dt=F32):
        return nc.dram_tensor(name, list(shape), dt, kind="ExternalInput").ap()

    x_loc = din("x_loc", [S, D])
    x_own = din("x_own", [NOWN, D])
    w_in = din("w_in", [D, c.INC])
    w_o = din("w_o", [D, D])
    w_gate = din("w_gate", [D, c.DFF])
    w_up = din("w_up", [D, c.DFF])
    w_down = din("w_down", [c.DFF, D])
    g_attn = din("g_attn", [128, D])
    g_ffn = din("g_ffn", [128, D])
    g_fin = din("g_fin", [128, D])
    g_sub = din("g_sub", [128, 256])
    bf_bc = din("bf_bc", [128, NFH])
    lam_bc = din("lam_bc", [128, 4 * 128])
    tab_bc = din("tab_bc", [128, NUM_BUCKETS * NDH])
    tri_in = din("tri", [128, 128])
    ohm_in = din("ohm", [128, (NUM_BUCKETS + 1) * 128])
    sel_in = din("sel", [128, NFH * 128])
    ident_in = din("ident", [128, 128])

    okind = "ExternalOutput"
    out_d = nc.dram_tensor("out", [NOWN, D], F32, kind=okind).ap()
    skind = "ExternalOutput" if debug else "Internal"
    VC = c.DW + c.FW
    KT_d = nc.dram_tensor("KT_d", [NHM, 128, S], BF16, kind=skind).ap()
    QT_d = nc.dram_tensor("QT_d", [NHM, 128, NOWN], BF16, kind=skind).ap()
    V_d = nc.dram_tensor("V_d", [S, VC], BF16, kind=skind).ap()
    mixT_d = nc.dram_tensor("mixT_d", [NCH, 128, NOWN], BF16, kind=skind).ap()
    x1_d = nc.dram_tensor("x1_d", [NOWN, D], F32, kind=skind).ap()
    cum_d = nc.dram_tensor("cum_d", [128, NB * NFH], F32, kind=skind).ap() if debug else None

    es = ExitStack()
    with es:
        B = Builder(nc, es)

        def sb(name, shape, dt):
            return es.enter_context(nc.sbuf_tensor(name, list(shape), dt))

        NS = 4
        RING_E = 8192
        ring = [sb(f"ring{i}", [128, RING_E], BF16) for i in range(NS)]
        BIG_E = 70656
        big = sb("big", [128, BIG_E], BF16)
        ident = sb("ident_sb", [128, 128], BF16)
        smallf = sb("smallf", [128, 64], F32)
        zero_c = smallf[:, 0:1]
        eps_c = smallf[:, 1:2]
        neglam = smallf[:, 2:3]
        ssb = smallf[:, 4:8]
        rsb = smallf[:, 8:12]
        lsc = smallf[:, 12:20]
        rlb = smallf[:, 20:28]
        nlb = smallf[:, 28:36]

        class Carver:
            def __init__(self, base=0):
                self.off = base

            def take(self, nbytes, dt, shape=None):
                nbytes = (nbytes + 3) // 4 * 4
                assert self.off % 4 == 0
                ne = nbytes // 2
                a = big[:, self.off // 2: self.off // 2 + ne]
                self.off += nbytes
                assert self.off <= BIG_E * 2, self.off
                if dt == F32:
                    a = a.bitcast(F32)
                return a

        TOP_BASE = BIG_E * 2 - 18432
        top = Carver(base=TOP_BASE)
        cumL = top.take(NB * NFH * 4, F32)
        gqT = top.take(NOWN * 2, BF16)
        sel = top.take(NFH * 128 * 2, BF16)
        tri = top.take(128 * 4, F32)
        ones_f = top.take(128 * 4, F32)
        ident_f = top.take(128 * 4, F32)
        tab = top.take(NUM_BUCKETS * NDH * 4, F32)
        gsub = top.take(256 * 4, F32)
        bfb = top.take(NFH * 4, F32)
        runL = top.take(NFH * 4, F32)
        Wf_sb = top.take(NCH * NFH * 2, BF16)
        lamv = top.take(4 * 128 * 4, F32)
        assert top.off <= BIG_E * 2

        P = [es.enter_context(nc.psum_tensor(f"P{i}", [128, 512], F32)) for i in range(6)]
        PT0 = es.enter_context(nc.psum_tensor("PT0", [128, 1024], BF16))
        P6 = es.enter_context(nc.psum_tensor("P6", [128, 512], F32))
        PT = [PT0[:, :], P6[:, :].bitcast(BF16)]
        SB = [P[0], P[1], P6]
        NSB = 3

        ring_ld = [B.sem(f"ring_ld{i}") for i in range(NS)]
        pe_piece = B.sem("pe_piece")
        st = {"piece": 0, "piece_tk": []}

        piece_srcs = []

        def req_piece(src_ap, shape3):
            i = st["piece"]; st["piece"] += 1
            s = i % NS
            a, b = shape3
            dst = ring[s][:, 0:a * b].rearrange("p (a b) -> p a b", a=a)
            waits = []
            if i >= NS:
                waits.append(st["piece_tk"][i - NS])
            tk = B.dma("pool", dst, src_ap, waits=waits, sig=ring_ld[s])
            return dst, tk

        def piece_done_sig():
            return pe_piece

        def piece_consumed(tk):
            st["piece_tk"].append(tk)


        setup_ld = B.sem("setup_ld")
        tks = []
        tks.append(B.dma("sp", tri, tri_in, sig=setup_ld))
        tks.append(B.dma("sp", ident_f, ident_in, sig=setup_ld))
        tks.append(B.dma("sp", tab, tab_bc, sig=setup_ld))
        tks.append(B.dma("sp", gsub, g_sub, sig=setup_ld))
        tks.append(B.dma("sp", bfb, bf_bc, sig=setup_ld))
        tks.append(B.dma("sp", lamv, lam_bc, sig=setup_ld))
        setup_c = B.sem("setup_c")
        tks.append(B.dma("pool", ident[:], ident_in, sig=setup_c))
        tks.append(B.dma("pool", sel[:, :], sel_in, sig=setup_c))
        wf_src = w_in[:, c.INC - NFH: c.INC].rearrange("(c p) f -> p c f", p=128)
        tks.append(B.dma("pool", Wf_sb.rearrange("p (c f) -> p c f", c=NCH), wf_src, sig=setup_c))
        setup_tk = [tks[5], tks[8]]

        dve_setup = B.sem("dve_setup")
        act_setup = B.sem("act_setup")
        B.op("dve", lambda e: e.memset(smallf[:, :], 0.0), waits=setup_tk)
        B.op("dve", lambda e: e.memset(eps_c, EPS))
        B.op("dve", lambda e: e.memset(ones_f, 1.0))
        B.op("dve", lambda e: e.memset(runL, 0.0))
        B.op("dve", lambda e: e.memset(gqT, 0.0))
        B.op("dve", lambda e: e.tensor_scalar(out=gsub, in0=gsub, scalar1=1.0 - LAM_INIT, scalar2=None, op0=ALU.mult))
        lv = lamv.rearrange("p (a d) -> p a d", a=4)
        junk128 = top.take(128 * 4, F32)
        B.op("dve", lambda e: e.scalar_tensor_tensor(out=junk128, in0=lv[:, 0, :], scalar=1.0, in1=lv[:, 1, :],
                                                     op0=ALU.mult, op1=ALU.mult, accum_out=lsc[:, 0:1]))
        tk_l = B.op("dve", lambda e: e.scalar_tensor_tensor(out=junk128, in0=lv[:, 2, :], scalar=1.0, in1=lv[:, 3, :],
                                                            op0=ALU.mult, op1=ALU.mult, accum_out=lsc[:, 1:2]),
                    sig=dve_setup)
        tk_e = B.op("act", lambda e: e.activation(out=lsc[:, 2:4], in_=lsc[:, 0:2], func=AF.Exp), waits=[tk_l],
                    sig=act_setup)
        B.op("dve", lambda e: e.tensor_tensor(out=lsc[:, 4:5], in0=lsc[:, 3:4], in1=lsc[:, 2:3], op=ALU.subtract),
             waits=[tk_e])
        tk_setup_dve = B.op("dve", lambda e: e.tensor_scalar(out=neglam, in0=lsc[:, 4:5], scalar1=-LAM_INIT, scalar2=None,
                                                             op0=ALU.add), sig=dve_setup)

        dve_ss = B.sem("dve_ss")
        act_rs = B.sem("act_rs")

        def rstd_act(ss_col, rs_col, inv_n, tk_ss):
            B.op("act", lambda e: e.activation(out=ss_col, in_=ss_col, func=AF.Ln, bias=eps_c, scale=inv_n), waits=[tk_ss])
            return B.op("act", lambda e: e.activation(out=rs_col, in_=ss_col, func=AF.Exp, scale=-0.5), sig=act_rs)

        xs_ld = [B.sem("xs_ld0"), B.sem("xs_ld1")]
        dve_norm = B.sem("dve_norm")
        pe_tp = B.sem("pe_tp")
        act_tp = B.sem("act_tp")
        nst = {"n": 0, "norm_tk": [], "tp_last": [], "tpg": 0, "tp_ev": []}

        def norm_transpose(xs, hn, g_tile, src_rows, hT_dst, extra_waits=(), g_wait=None, mid_hook=None):
            n = nst["n"]; nst["n"] += 1
            b = n % 2
            w = list(extra_waits)
            back = 1 if xs[0] is xs[1] else 2
            if n >= back:
                w.append(nst["norm_tk"][n - back])
            tk_ld = B.dma("sp", xs[b], src_rows, waits=w, sig=xs_ld[b])
            w = [tk_ld, tk_setup_dve]
            if n >= 2:
                w.append(nst["tp_last"][n - 2])
            tk_ss = B.op("dve", lambda e: e.scalar_tensor_tensor(out=hn[b], in0=xs[b], scalar=1.0, in1=xs[b], op0=ALU.mult,
                                                                 op1=ALU.mult, accum_out=ssb[:, b:b + 1]), waits=w, sig=dve_ss)
            tk_rs = rstd_act(ssb[:, b:b + 1], rsb[:, b:b + 1], 1.0 / D, tk_ss)
            tk_n = B.op("dve", lambda e: e.scalar_tensor_tensor(out=hn[b], in0=xs[b], scalar=rsb[:, b:b + 1], in1=g_tile,
                                                                op0=ALU.mult, op1=ALU.mult),
                        waits=[g_wait, tk_rs], sig=dve_norm)
            nst["norm_tk"].append(tk_n)
            if mid_hook is not None:
                mid_hook()
            last = None
            ev = None
            for c0 in range(0, NCH, 8):
                ng = min(8, NCH - c0)
                gi = nst["tpg"]; nst["tpg"] += 1
                pb = PT[gi % 2]
                w = [tk_n]
                if gi >= 2:
                    w.append(nst["tp_ev"][gi - 2])
                for k in range(ng):
                    cc = c0 + k
                    is_last = (k == ng - 1)
                    fn = (lambda e, cc=cc, k=k, pb=pb: e.transpose(out=pb[:, k * 128:(k + 1) * 128],
                                                                    in_=hn[b][:, cc * 128:(cc + 1) * 128], identity=ident[:]))
                    if is_last:
                        last = B.op("pe", fn, waits=w if k == 0 else (), sig=pe_tp)
                    else:
                        B.op("pe", fn, waits=w if k == 0 else ())
                dst = hT_dst(c0, ng)
                src = pb[:, 0:ng * 128].rearrange("p (a b) -> p a b", a=ng)
                ev = B.op("act", lambda e, dst=dst, src=src: e.copy(out=dst, in_=src), waits=[last], sig=act_tp)
                nst["tp_ev"].append(ev)
            nst["tp_last"].append(last)
            return ev

        ca = Carver()
        NBA = c.NBA
        TA = NBA * 128
        hT = ca.take(NCH * TA * 2, BF16)
        hT4 = hT.rearrange("p (c t q) -> p c t q", c=NCH, t=NBA)
        xsA0 = ca.take(D * 4, F32)
        xsA = [xsA0, xsA0]
        hnA = [ca.take(D * 2, BF16) for _ in range(2)]
        gA = ca.take(D * 4, F32)
        NSTG = 3
        stgK = [ca.take(512 * 2, BF16) for _ in range(NSTG)]
        zf = ca.take(NFH * 4, F32)
        ef = ca.take(NFH * 4, F32)
        lf = ca.take(NFH * 4, F32)
        assert ca.off <= TOP_BASE, ca.off

        gA_ld = B.sem("gA_ld")
        tk_gA = B.dma("sp", gA, g_attn, sig=gA_ld)

        pe_proj = B.sem("pe_proj")
        act_ev = B.sem("act_ev")
        stg_st = [B.sem(f"stg_st{i}") for i in range(NSTG)]
        pe_f = B.sem("pe_f")
        dve_f = B.sem("dve_f")
        act_f = B.sem("act_f")
        pe_c = B.sem("pe_c")
        dve_c = B.sem("dve_c")
        pst = {"grp": 0, "ev_tk": [], "stg_n": 0, "stg_tk": [None] * NSTG, "last_mm": None}

        def proj_group(mm_list, ncols, dst_dram, split=None, defer=None):
            gi = pst["grp"]; pst["grp"] += 1
            bank = P[gi % 3]
            w = []
            if gi >= 3:
                w.append(pst["ev_tk"][gi - 3])
            nmm = len(mm_list)
            tk = None
            for k, (lhsT, rhs, extra_w) in enumerate(mm_list):
                ww = list(w) if k == 0 else []
                ww += list(extra_w)
                ov = bank[:, 0:ncols] if split is None else bank[:, 0:ncols].rearrange("p (a b) -> p a b", a=split)
                fn = (lambda e, lhsT=lhsT, rhs=rhs, k=k, ov=ov: e.matmul(ov, lhsT=lhsT, rhs=rhs,
                                                                         start=(k == 0), stop=(k == nmm - 1)))
                if k == nmm - 1:
                    tk = B.op("pe", fn, waits=ww, sig=pe_proj)
                else:
                    B.op("pe", fn, waits=ww)
            def emit_evac():
                assert len(pst["ev_tk"]) == gi
                si = pst["stg_n"] % NSTG; pst["stg_n"] += 1
                stg = stgK[si][:, 0:ncols]
                tk_ev = B.op("act", lambda e: e.copy(out=stg, in_=bank[:, 0:ncols]), waits=[tk, pst["stg_tk"][si]],
                             sig=act_ev)
                pst["ev_tk"].append(tk_ev)
                pst["stg_tk"][si] = B.dma("sp", dst_dram, stg, waits=[tk_ev], sig=stg_st[si])

            if defer is None:
                emit_evac()
            else:
                defer.append(emit_evac)
            return tk

        def w_in_piece(col0):
            src = w_in[:, col0:col0 + c.PC].rearrange("(c p) f -> p c f", p=128)
            return req_piece(src, (NCH, c.PC))

        C_DQ, C_DK, C_DV = 0, c.DW, 2 * c.DW
        C_FQ, C_FK, C_FV = 3 * c.DW, 3 * c.DW + c.FW, 3 * c.DW + 2 * c.FW
        Wf3 = Wf_sb.rearrange("p (c f) -> p c f", c=NCH)
        cumL3 = cumL.rearrange("p (b h) -> p b h", b=NB)

        hT_free = None
        for t in range(c.NTA):
            tk_hT = None
            v_pieces = [(cb, vc0, pc0) for (cb, vc0, width) in ((C_DV, 0, c.DW), (C_FV, c.DW, c.FW))
                        for pc0 in range(0, width, c.PC)]
            NV1 = min(3, len(v_pieces))
            early = []
            early_last = [None] * NV1
            deferred = []

            def flush_deferred():
                while deferred:
                    deferred.pop(0)()

            for tb in range(NBA):
                blk = t * NBA + tb
                tk_blk = norm_transpose(
                    xsA, hnA, gA, x_loc[blk * 128:(blk + 1) * 128, :],
                    (lambda c0, n, tb=tb: hT4[:, c0:c0 + n, tb, :]),
                    extra_waits=[hT_free] if hT_free is not None else [], g_wait=tk_gA, mid_hook=flush_deferred)
                tk_hT = tk_blk
                if tb == 0:
                    for (cb, vc0, pc0) in v_pieces[:NV1]:
                        pv, tkp = w_in_piece(cb + pc0)
                        early.append((pv, tkp, vc0 + pc0))
                r0 = t * TA + tb * 128
                for ei, (pv, tkp, vcol) in enumerate(early):
                    mm = [(hT4[:, cc, tb, :], pv[:, cc, :], [tkp, tk_blk] if cc == 0 else []) for cc in range(NCH)]
                    early_last[ei] = proj_group(mm, c.PC, V_d[r0:r0 + 128, vcol: vcol + c.PC], defer=deferred)
            flush_deferred()
            for ei in range(NV1):
                piece_consumed(early_last[ei])
            hw = [tk_hT]
            for tb in range(NBA):
                blk = t * NBA + tb
                for cc in range(NCH):
                    fn = (lambda e, cc=cc, tb=tb: e.matmul(P[3][:, 0:NFH], lhsT=hT4[:, cc, tb, :], rhs=Wf3[:, cc, :],
                                                            start=(cc == 0), stop=(cc == NCH - 1)))
                    if cc == NCH - 1:
                        tk_f = B.op("pe", fn, sig=pe_f)
                    else:
                        w = list(hw) if cc == 0 else []
                        if cc == 0 and blk > 0:
                            w.append(pst["zf_tk"])
                        B.op("pe", fn, waits=w)
                tkz = B.op("dve", lambda e: e.tensor_tensor(out=zf, in0=P[3][:, 0:NFH], in1=bfb, op=ALU.add),
                           waits=[tk_f, pst.get("lf_free")], sig=dve_f)
                pst["zf_tk"] = tkz
                B.op("act", lambda e: e.activation(out=ef, in_=zf, func=AF.Exp, scale=-1.0), waits=[tkz, pst.get("c_tk")])
                tkl = B.op("act", lambda e: e.activation(out=lf, in_=ef, func=AF.Ln, bias=1.0, scale=1.0), sig=act_f)
                B.op("pe", lambda e: e.matmul(P[4][:, 0:NFH], lhsT=tri, rhs=lf, start=True, stop=True),
                     waits=[tkl, pst.get("cum_ev")])
                tkc = B.op("pe", lambda e: e.matmul(P[4][:, 64:64 + NFH], lhsT=ones_f, rhs=lf, start=True, stop=True),
                           sig=pe_c)
                pst["c_tk"] = tkc
                B.op("dve", lambda e, blk=blk: e.tensor_tensor(out=cumL3[:, blk, :], in0=P[4][:, 0:NFH], in1=runL,
                                                               op=ALU.add), waits=[tkc])
                tkr = B.op("dve", lambda e: e.tensor_tensor(out=runL, in0=P[4][:, 64:64 + NFH], in1=runL, op=ALU.add),
                           sig=dve_c)
                pst["lf_free"] = tkc
                tkg = B.op("pe", lambda e, blk=blk: e.matmul(P[5][0:NFH, 0:128], lhsT=cumL3[:, blk, :], rhs=ident_f,
                                                              start=True, stop=True),
                           waits=[tkr, pst.get("gq_ev")], sig=pe_c)
                tkge = B.op("dve", lambda e, blk=blk: e.tensor_scalar(out=gqT[0:NFH, blk * 64:(blk + 1) * 64],
                                                                       in0=P[5][0:NFH, 0:64], scalar1=-1.0 / c.SCALE,
                                                                       scalar2=None, op0=ALU.mult),
                            waits=[tkg], sig=dve_c)
                pst["cum_ev"] = tkr
                pst["gq_ev"] = tkge
            last_tk = None
            for (cb, hm0) in ((C_DQ, 0), (C_FQ, 2 * NDH)):
                ncolblk = (c.DW if cb == C_DQ else c.FW) // 128
                for pc0 in range(0, ncolblk * 128, c.PC):
                    pv, tkp = w_in_piece(cb + pc0)
                    for j in range(c.PC // 128):
                        hm = hm0 + (pc0 // 128) + j
                        for q0 in range(0, NBA, 8):
                            nq = min(8, NBA - q0)
                            mm = [(pv[:, cc, j * 128:(j + 1) * 128], hT4[:, cc, q0:q0 + nq, 0:64],
                                   ([tkp] + hw) if cc == 0 else []) for cc in range(NCH)]
                            o0 = t * NBA * 64 + q0 * 64
                            last_tk = proj_group(mm, nq * 64, QT_d[hm, :, o0:o0 + nq * 64], split=nq)
                    piece_consumed(last_tk)
            for (cb, hm0) in ((C_DK, 0), (C_FK, 2 * NDH)):
                ncolblk = (c.DW if cb == C_DK else c.FW) // 128
                for pc0 in range(0, ncolblk * 128, c.PC):
                    pv, tkp = w_in_piece(cb + pc0)
                    for j in range(c.PC // 128):
                        hm = hm0 + (pc0 // 128) + j
                        for k0 in range(0, NBA, 4):
                            mm = [(pv[:, cc, j * 128:(j + 1) * 128], hT4[:, cc, k0:k0 + 4, :],
                                   ([tkp] + hw) if cc == 0 else []) for cc in range(NCH)]
                            o0 = t * TA + k0 * 128
                            last_tk = proj_group(mm, 512, KT_d[hm, :, o0:o0 + 512], split=4)
                    piece_consumed(last_tk)
            for (cb, vc0, pc0) in v_pieces[NV1:]:
                pv, tkp = w_in_piece(cb + pc0)
                for tb in range(NBA):
                    r0 = t * TA + tb * 128
                    mm = [(hT4[:, cc, tb, :], pv[:, cc, :], ([tkp] + hw) if cc == 0 else []) for cc in range(NCH)]
                    last_tk = proj_group(mm, c.PC, V_d[r0:r0 + 128, vc0 + pc0: vc0 + pc0 + c.PC])
                piece_consumed(last_tk)
            hT_free = last_tk

        if debug:
            B.dma("sp", cum_d, cumL, waits=[pst["cum_ev"]], sig=setup_ld)
        B.barrier()

        cb_ = Carver()
        KTs = [cb_.take(S * 2, BF16) for _ in range(2)]
        QTs = [cb_.take(NOWN * 2, BF16) for _ in range(2)]
        VdS = [cb_.take(NB * 257 * 2 + 2, BF16) for _ in range(2)]
        vf_off = cb_.off
        VfS = [cb_.take(NB * 129 * 2 + 2, BF16) for _ in range(2)]
        assert c.NG * 4 * 256 * 4 <= cb_.off - vf_off, "O1 alias does not fit in the fox V buffers"
        O1 = big[:, vf_off // 2: vf_off // 2 + c.NG * 4 * 256 * 2].bitcast(F32)
        Wt = [cb_.take(576 * 4, F32) for _ in range(NDH + 1)]
        NPT = 4
        PTs = [cb_.take(512 * 2, BF16) for _ in range(NPT)]
        tmpS = [cb_.take(512 * 4, F32) for _ in range(3)]
        Of = cb_.take(256 * 4, F32)
        On = [cb_.take(256 * 2, BF16) for _ in range(4)]
        mixS = [cb_.take(2 * 512 * 2, BF16) for _ in range(2)]
        ohm = cb_.take((NUM_BUCKETS + 1) * 128 * 2, BF16)
        assert cb_.off <= TOP_BASE, cb_.off
        O14 = O1.rearrange("p (g a d) -> p g a d", g=c.NG, a=4)

        ohm_ld = B.sem("ohm_ld")
        tk_ohm = B.dma("pool", ohm, ohm_in, sig=ohm_ld)
        ohm3 = ohm.rearrange("p (b q) -> p b q", b=NUM_BUCKETS + 1)
        dve_w = B.sem("dve_w")
        tk_w = None
        for h in range(NDH):
            W = Wt[h]
            B.op("dve", lambda e, W=W: e.tensor_copy(out=W[:, 0:128], in_=ohm3[:, NUM_BUCKETS, :]), waits=[tk_ohm])
            for bk in range(NUM_BUCKETS):
                B.op("dve", lambda e, W=W, bk=bk, h=h: e.scalar_tensor_tensor(
                    out=W[:, 0:128], in0=ohm3[:, bk, :], scalar=tab[:, bk * NDH + h: bk * NDH + h + 1], in1=W[:, 0:128],
                    op0=ALU.mult, op1=ALU.add))
            B.op("dve", lambda e, W=W: e.memset(W[:, 128:576], 0.0))
            tk_w = B.op("dve", lambda e, W=W, h=h: e.tensor_scalar(
                out=W[:, 128:576], in0=W[:, 128:576], scalar1=tab[:, 31 * NDH + h: 31 * NDH + h + 1], scalar2=None,
                op0=ALU.add), sig=dve_w)
        W = Wt[NDH]
        B.op("dve", lambda e: e.memset(W[:, :], 0.0))
        tk_w = B.op("dve", lambda e: e.tensor_copy(out=W[:, 0:64], in_=ohm3[:, NUM_BUCKETS, 0:64]), sig=dve_w)
        for bi in range(2):
            vd3 = VdS[bi][:, 0:NB * 257].rearrange("p (b d) -> p b d", b=NB)
            vf3 = VfS[bi][:, 0:NB * 129].rearrange("p (b d) -> p b d", b=NB)
            tk_w = B.op("dve", lambda e, vd3=vd3: e.memset(vd3[:, :, 256:257], 1.0))

        kq_ld = [B.sem("kq_ld0"), B.sem("kq_ld1")]
        vd_ld = [B.sem("vd_ld0"), B.sem("vd_ld1")]
        vf_ld = [B.sem("vf_ld0"), B.sem("vf_ld1")]
        pe_s = B.sem("pe_s")
        dve_pre = B.sem("dve_pre")
        act_exp = B.sem("act_exp")
        pe_pv = B.sem("pe_pv")
        pe_o = B.sem("pe_o")
        dve_o = B.sem("dve_o")
        pe_ot = B.sem("pe_ot")
        act_ot = B.sem("act_ot")
        mix_st = [B.sem("mix_st0"), B.sem("mix_st1")]

        V_d3 = V_d.rearrange("(b p) v -> p b v", p=128)
        ast = {"n": 0, "s_tk": {}, "exp_tk": {}, "pv_tk": {}, "unit": 0, "kq_free": [None, None],
               "vd_n": 0, "vf_n": 0, "vd_free": [None, None], "vf_free": [None, None], "o_ev": None,
               "on_n": 0, "ot_ev": [None, None, None, None], "ot_n": 0, "mix_n": 0, "mix_free": [None, None], "pre_tk": {}}

        units = []
        for h in range(NDH):
            units.append(("d", h, 0)); units.append(("d", h, 1))
        for h in range(NFH):
            units.append(("f", h, 0))

        def load_unit(ui):
            kind, h, m = units[ui]
            b = ui % 2
            hm = (2 * h + m) if kind == "d" else (2 * NDH + h)
            w = [ast["kq_free"][b]]
            B.dma("sp", KTs[b], KT_d[hm], waits=w, sig=kq_ld[b])
            tk = B.dma("sp", QTs[b], QT_d[hm], sig=kq_ld[b])
            tkv = None
            if kind == "d" and m == 0:
                vb = ast["vd_n"] % 2; ast["vd_n"] += 1
                vd3 = VdS[vb][:, 0:NB * 257].rearrange("p (b d) -> p b d", b=NB)
                tkv = B.dma("sp", vd3[:, :, 0:256], V_d3[:, :, h * 256:(h + 1) * 256],
                            waits=[ast["vd_free"][vb], tk_w], sig=vd_ld[vb])
                ast["cur_vd"] = (vb, tkv)
            elif kind == "f":
                if ast.get("diff_final") is None:
                    ast.setdefault("deferred", []).append(ui)
                else:
                    load_vf(ui)
            return tk

        def load_vf(ui):
            kind, h, m = units[ui]
            vb = ast["vf_n"] % 2; ast["vf_n"] += 1
            vf3 = VfS[vb][:, 0:NB * 129].rearrange("p (b d) -> p b d", b=NB)
            if ast["vf_n"] <= 2:
                tko = B.op("dve", lambda e, vf3=vf3: e.memset(vf3[:, :, 128:129], 1.0), waits=[ast["diff_final"]])
            else:
                tko = None
            tkv = B.dma("sp", vf3[:, :, 0:128], V_d3[:, :, c.DW + h * 128: c.DW + (h + 1) * 128],
                        waits=[ast["vf_free"][vb], tk_w, ast["diff_final"], tko], sig=vf_ld[vb])
            unit_v[ui] = (vb, tkv, tko)

        unit_ld = {}
        unit_v = {}

        def issue_load(ui):
            if ui < len(units) and ui not in unit_ld:
                unit_ld[ui] = load_unit(ui)
                kind, h, m = units[ui]
                if kind == "d":
                    unit_v[ui] = ast["cur_vd"] + (None,)

        issue_load(0)
        issue_load(1)

        for ui, (kind, h, m) in enumerate(units):
            b = ui % 2
            tk_kq = unit_ld[ui]
            vb, tk_v, tk_ones = unit_v[ui]
            O13 = O14[:, 0, :, :]
            KT = KTs[b]; QT = QTs[b]
            if kind == "d":
                V3 = VdS[vb][:, 0:NB * 257].rearrange("p (b d) -> p b d", b=NB)
                NV = 257
                Wm = Wt[h]
            else:
                V3 = VfS[vb][:, 0:NB * 129].rearrange("p (b d) -> p b d", b=NB)
                NV = 129
                Wm = Wt[NDH]
            last_pe = None
            for g in range(c.NG):
                O13 = O14[:, g, :, :]
                nj = 8 * g + 8
                tiles = []
                for j in range(nj):
                    mrel = j - 8 * g
                    lo = 64 * mrel if mrel >= 0 else 0
                    n = 512 - lo
                    if kind == "d":
                        special = (mrel >= -1)
                        woff = 64 if mrel == -1 else 0
                    else:
                        special = (mrel >= 0)
                        woff = 0
                    tiles.append((j, lo, n, special, woff))

                def emit_S(idx):
                    j, lo, n, special, woff = tiles[idx]
                    k = ast["n"] + idx
                    sbk = SB[k % NSB]
                    w = [tk_kq, tk_setup_dve, ast.get(("ptev", 0))]
                    if k >= NSB:
                        w.append(ast["exp_tk"].get(k - NSB))
                        w.append(ast["pre_tk"].get(k - NSB))
                    lhsT = KT[:, j * 128:(j + 1) * 128]
                    rhs = QT[:, g * 512 + lo: (g + 1) * 512]
                    if kind == "d":
                        ast["s_tk"][k] = B.op("pe", lambda e: e.matmul(sbk[:, 0:n], lhsT=lhsT, rhs=rhs, start=True, stop=True),
                                              waits=w, sig=pe_s)
                    else:
                        B.op("pe", lambda e: e.matmul(sbk[:, 0:n], lhsT=lhsT, rhs=rhs, start=True, stop=False), waits=w)
                        lhs2 = sel[:, h * 128:(h + 1) * 128]
                        rhs2 = gqT[:, g * 512 + lo:(g + 1) * 512]
                        ast["s_tk"][k] = B.op("pe", lambda e: e.matmul(sbk[:, 0:n], lhsT=lhs2, rhs=rhs2, start=False,
                                                                        stop=True), sig=pe_s)

                def emit_exp(idx):
                    j, lo, n, special, woff = tiles[idx]
                    k = ast["n"] + idx
                    sbk = SB[k % NSB]
                    pt = PTs[k % NPT]
                    wpt = ast["pv_tk"].get(k - NPT)
                    if kind == "d":
                        bias_full = tab[:, 31 * NDH + h: 31 * NDH + h + 1]
                        bias_sp = zero_c
                    else:
                        bias_full = cumL3[:, j, h:h + 1]
                        bias_sp = bias_full
                    if special:
                        if kind == "d":
                            wc = min(n, 128) if woff == 0 else 64
                        else:
                            wc = 64
                        wc = n
                        tm = tmpS[k % 3]
                        wv = Wm[:, woff:woff + wc]
                        tkp = B.op("dve", lambda e: e.scalar_tensor_tensor(
                            out=tm[:, 0:wc], in0=sbk[:, 0:wc], scalar=c.SCALE, in1=wv, op0=ALU.mult,
                            op1=ALU.add), waits=[ast["s_tk"][k], ast["exp_tk"].get(k - 3), tk_w], sig=dve_pre, indep=True)
                        ast["pre_tk"][k] = tkp
                        if n > wc:
                            B.op("act", lambda e: e.activation(out=pt[:, wc:n], in_=sbk[:, wc:n], func=AF.Exp,
                                                               bias=bias_full, scale=c.SCALE),
                                 waits=[ast["s_tk"][k], wpt], sig=act_exp, indep=True)
                        ast["exp_tk"][k] = B.op("act", lambda e: e.activation(out=pt[:, 0:wc], in_=tm[:, 0:wc], func=AF.Exp,
                                                                               bias=bias_sp, scale=1.0),
                                                waits=[tkp, wpt], sig=act_exp, indep=True)
                    else:
                        ast["exp_tk"][k] = B.op("act", lambda e: e.activation(out=pt[:, 0:n], in_=sbk[:, 0:n], func=AF.Exp,
                                                                               bias=bias_full, scale=c.SCALE),
                                                waits=[ast["s_tk"][k], wpt], sig=act_exp, indep=True)

                def emit_pv(idx):
                    j, lo, n, special, woff = tiles[idx]
                    k = ast["n"] + idx
                    pt = PTs[k % NPT]
                    tk = None
                    first = True
                    for qc in range(4):
                        c_lo = max(128 * qc, lo)
                        c_hi = 128 * qc + 128
                        if c_lo >= c_hi:
                            continue
                        r0 = c_lo - 128 * qc
                        i1 = 8 * g + 2 * qc + 1
                        w = [ast["exp_tk"][k], tk_v, tk_ones] if first else []
                        if j == 0:
                            w.append(ast["o_ev"])
                        first = False
                        rv_ = V3[:, j, 0:NV]
                        if j == i1 - 1 and r0 == 0 and not os.environ.get('NOSPLITMM'):
                            ov = P[2 + qc][0:64, 0:NV]
                            lv_ = pt[:, c_lo - lo:c_lo - lo + 64]
                            B.op("pe", lambda e, ov=ov, lv_=lv_, rv_=rv_, st_=(j == 0): e.matmul(
                                ov, lhsT=lv_, rhs=rv_, start=st_, stop=True), waits=w)
                            w = []
                            ov = P[2 + qc][64:128, 0:NV]
                            lv_ = pt[:, c_lo - lo + 64:c_hi - lo]
                            tk = B.op("pe", lambda e, ov=ov, lv_=lv_, rv_=rv_, st_=(j == 0): e.matmul(
                                ov, lhsT=lv_, rhs=rv_, start=st_, stop=False), waits=w, sig=pe_pv)
                        else:
                            ov = P[2 + qc][r0:128, 0:NV]
                            lv_ = pt[:, c_lo - lo:c_hi - lo]
                            tk = B.op("pe", lambda e, ov=ov, lv_=lv_, rv_=rv_, st_=(j == 0), sp_=(j == i1): e.matmul(
                                ov, lhsT=lv_, rhs=rv_, start=st_, stop=sp_), waits=w, sig=pe_pv)
                    ast["pv_tk"][k] = tk
                    return tk

                nt = len(tiles)
                for i_ in range(min(NSB, nt)):
                    emit_S(i_)
                p2a_at = 1
                p2b_at = min(nt - 1, 12)
                for idx in range(nt):
                    emit_exp(idx)
                    if idx + NSB < nt:
                        emit_S(idx + NSB)
                    last_pe = emit_pv(idx)
                    if idx == p2a_at and ast.get("pend_a"):
                        ast["pend_a"](); ast["pend_a"] = None
                    if idx == p2b_at and ast.get("pend_b"):
                        ast["pend_b"](); ast["pend_b"] = None
                ast["n"] += nt
                tk_o = last_pe

                if kind == "d" and m == 0:
                    tkr = []
                    for qc in range(4):
                        tkr.append(B.op("dve", lambda e, qc=qc: e.reciprocal(out=rlb[:, qc:qc + 1], in_=P[2 + qc][:, 256:257]),
                                        waits=[tk_o, ast.get("o1_free")], sig=dve_o, indep=(qc > 0)))
                    for qc in range(4):
                        tk_e = B.op("dve", lambda e, qc=qc, O13=O13: e.tensor_scalar(out=O13[:, qc, :], in0=P[2 + qc][:, 0:256],
                                                                            scalar1=rlb[:, qc:qc + 1], scalar2=None,
                                                                            op0=ALU.mult), waits=[tkr[qc]], sig=dve_o, indep=True)
                    ast["o_ev"] = tk_e
                else:
                    mi = ast["mix_n"] % 2; ast["mix_n"] += 1
                    nchk = 2 if kind == "d" else 1
                    ms3 = mixS[mi].rearrange("p (a q) -> p a q", a=2)
                    ons = []
                    for qc in range(4):
                        oi = ast["on_n"] % 4; ast["on_n"] += 1
                        ons.append(oi)
                    if kind == "d":
                        tkr = []
                        tkn = []
                        for qc in range(4):
                            tkr.append(B.op("dve", lambda e, qc=qc: e.reciprocal(out=rlb[:, qc:qc + 1], in_=P[2 + qc][:, 256:257]),
                                            waits=[tk_o], sig=dve_o, indep=(qc > 0)))
                        for qc in range(4):
                            tkn.append(B.op("dve", lambda e, qc=qc: e.tensor_scalar(out=nlb[:, qc:qc + 1], in0=rlb[:, qc:qc + 1],
                                                                                    scalar1=neglam, scalar2=None, op0=ALU.mult),
                                            waits=[tkr[qc]], sig=dve_o, indep=True))
                        for qc in range(4):
                            tk_e = B.op("dve", lambda e, qc=qc, O13=O13: e.scalar_tensor_tensor(
                                out=O13[:, qc, :], in0=P[2 + qc][:, 0:256], scalar=nlb[:, qc:qc + 1], in1=O13[:, qc, :],
                                op0=ALU.mult, op1=ALU.add), waits=[tkn[qc]], sig=dve_o, indep=True)
                        tk_on_list = [None] * 4
                    else:
                        tk_on_list = []
                        for qc in range(4):
                            on = On[ons[qc]]
                            B.op("dve", lambda e, qc=qc: e.reciprocal(out=rlb[:, qc:qc + 1], in_=P[2 + qc][:, 128:129]),
                                 waits=[tk_o])
                            tk_e = B.op("dve", lambda e, qc=qc, on=on: e.tensor_scalar(
                                out=on[:, 0:128], in0=P[2 + qc][:, 0:128], scalar1=rlb[:, qc:qc + 1], scalar2=None,
                                op0=ALU.mult), waits=[ast["ot_ev"][ons[qc]]], sig=dve_o)
                            tk_on_list.append(tk_e)
                    ast["o_ev"] = tk_e
                    if kind == "d":
                        ch0 = 2 * h
                    else:
                        ch0 = 2 * NDH + h
                    dstm = mixT_d[ch0:ch0 + nchk, :, g * 512:(g + 1) * 512].rearrange("a p q -> p a q")

                    def phase2a(kind=kind, O13=O13, ons=ons, tk_on_list=tk_on_list):
                        if kind != "d":
                            return
                        for qc in range(4):
                            on = On[ons[qc]]
                            tk_s = B.op("dve", lambda e, qc=qc, O13=O13: e.scalar_tensor_tensor(
                                out=Of, in0=O13[:, qc, :], scalar=1.0, in1=O13[:, qc, :], op0=ALU.mult, op1=ALU.mult,
                                accum_out=ssb[:, 2:3]), sig=dve_o)
                            tk_rs = rstd_act(ssb[:, 2:3], rsb[:, 2:3], 1.0 / 256, tk_s)
                            tk_on_list[qc] = B.op("dve", lambda e, qc=qc, on=on, O13=O13: e.scalar_tensor_tensor(
                                out=on[:, 0:256], in0=O13[:, qc, :], scalar=rsb[:, 2:3], in1=gsub, op0=ALU.mult,
                                op1=ALU.mult), waits=[ast["ot_ev"][ons[qc]], tk_rs], sig=dve_o)
                        ast["o1_free"] = tk_on_list[3]

                    def phase2b(nchk=nchk, ons=ons, tk_on_list=tk_on_list, ms3=ms3, mi=mi, dstm=dstm):
                        tk_last_ev = None
                        for qc in range(4):
                            on = On[ons[qc]]
                            ti = 0
                            pb = PT[ti]
                            w = [tk_on_list[qc], ast.get(("ptev", ti))]
                            tkt = None
                            for a in range(nchk):
                                tkt = B.op("pe", lambda e, a=a, on=on, pb=pb: e.transpose(
                                    out=pb[:, a * 128:(a + 1) * 128], in_=on[:, a * 128:(a + 1) * 128], identity=ident[:]),
                                    waits=w if a == 0 else (), sig=pe_ot)
                            ast["ot_ev"][ons[qc]] = tkt
                            src = pb[:, 0:nchk * 128].rearrange("p (a q) -> p a q", a=nchk)
                            dstv = ms3[:, 0:nchk, qc * 128:(qc + 1) * 128]
                            tk_last_ev = B.op("act", lambda e, src=src, dstv=dstv: e.copy(out=dstv, in_=src),
                                              waits=[tkt, ast["mix_free"][mi]], sig=act_ot)
                            ast[("ptev", ti)] = tk_last_ev
                        ast["mix_free"][mi] = B.dma("sp", dstm, ms3[:, 0:nchk, :], waits=[tk_last_ev], sig=mix_st[mi])

                    if ast.get("pend_a"):
                        ast["pend_a"](); ast["pend_a"] = None
                    if ast.get("pend_b"):
                        ast["pend_b"](); ast["pend_b"] = None
                    ast["pend_a"] = phase2a
                    ast["pend_b"] = phase2b
                    if os.environ.get("NODEFER"):
                        ast["pend_a"](); ast["pend_a"] = None
                        ast["pend_b"](); ast["pend_b"] = None
                    if kind == "d" and ui == 2 * NDH - 1 and g == c.NG - 1:
                        if ast.get("pend_a"):
                            ast["pend_a"](); ast["pend_a"] = None
                        if ast.get("pend_b"):
                            ast["pend_b"](); ast["pend_b"] = None
            ast["kq_free"][b] = last_pe
            if kind == "d" and m == 1:
                ast["vd_free"][vb] = last_pe
            if kind == "f":
                ast["vf_free"][vb] = last_pe
            issue_load(ui + 2)
            if ui == 2 * NDH - 1:
                ast["diff_final"] = ast["o1_free"]
                for du in ast.get("deferred", []):
                    load_vf(du)

        if ast.get("pend_a"):
            ast["pend_a"](); ast["pend_a"] = None
        if ast.get("pend_b"):
            ast["pend_b"](); ast["pend_b"] = None
        B.barrier()

        cc_ = Carver()
        actT = cc_.take(NFC * 512 * 2, BF16)
        U = cc_.take(NCH * 512 * 2, BF16)
        NXS = 3
        xst = [cc_.take(512 * 4, F32) for _ in range(NXS)]
        ost = [cc_.take(512 * 4, F32) for _ in range(NXS)]
        sgs = [cc_.take(512 * 4, F32) for _ in range(2)]
        junkF = cc_.take(512 * 4, F32)
        assert cc_.off <= BIG_E * 2, cc_.off
        actT3 = actT.rearrange("p (f t) -> p f t", f=NFC)
        mixT = big[:, 0:NCH * 512]
        mixT3 = mixT.rearrange("p (c t) -> p c t", c=NCH)
        o2 = NCH * 512
        xsC0 = big[:, o2: o2 + 2 * D].bitcast(F32)
        xsC = [xsC0, xsC0]
        o3 = o2 + 2 * D
        hnC = [big[:, o3 + i * D: o3 + (i + 1) * D] for i in range(2)]
        o4 = o3 + 2 * D
        gF = big[:, o4: o4 + 2 * D].bitcast(F32)
        assert o4 + 2 * D <= NFC * 512, "alias overflow"
        h2T = U
        h2T4 = h2T.rearrange("p (c t q) -> p c t q", c=NCH, t=4)
        Ub = NFC * 512
        gFin = big[:, Ub: Ub + 2 * D].bitcast(F32)
        xrow = big[:, Ub + 2 * D: Ub + 4 * D].bitcast(F32)
        assert 4 * D <= NCH * 512

        gF_ld = B.sem("gF_ld")
        mix_ld = B.sem("mix_ld")
        xst_ld = [B.sem(f"xst_ld{i}") for i in range(NXS)]
        ost_st = [B.sem(f"ost_st{i}") for i in range(NXS)]
        pe_acc = B.sem("pe_acc")
        dve_ev = B.sem("dve_ev")
        pe_gu = B.sem("pe_gu")
        act_silu = B.sem("act_silu")
        dve_act = B.sem("dve_act")
        gfin_ld = B.sem("gfin_ld")
        xrow_ld = B.sem("xrow_ld")
        xrow_st = B.sem("xrow_st")
        dve_fin = B.sem("dve_fin")
        cst = {"xs_n": 0, "xst_free": [None] * NXS, "ost_free": [None] * NXS, "acc_ev": [None] * 4, "gu_n": 0,
               "gu_ev": {}, "sg_free": [None, None], "fin_tk": None, "xrow_free": None}

        def acc_phase(t, wsrc_fn, nrows_chunks, lhs_fn, lhs_waits, res_dram, dst_dram, after_n=None):
            last = None
            for n in range(D // 512):
                pieces = []
                for k0 in range(0, nrows_chunks, c.KC):
                    pieces.append((k0, min(c.KC, nrows_chunks - k0)))
                tks = [None] * 4
                for pi, (k0, kc) in enumerate(pieces):
                    pv, tkp = req_piece(wsrc_fn(k0, kc, n), (kc, 512))
                    for tb in range(4):
                        for k in range(kc):
                            ch = k0 + k
                            first = (ch == 0); lastk = (ch == nrows_chunks - 1)
                            w = []
                            if k == 0:
                                w = [tkp] + list(lhs_waits)
                                if first:
                                    w.append(cst["acc_ev"][tb])
                            fn = (lambda e, tb=tb, ch=ch, k=k, pv=pv, first=first, lastk=lastk: e.matmul(
                                P[tb][:, :], lhsT=lhs_fn(ch, tb), rhs=pv[:, k, :], start=first, stop=lastk))
                            if lastk or (tb == 3 and k == kc - 1):
                                tk = B.op("pe", fn, waits=w, sig=pe_acc)
                                if lastk:
                                    tks[tb] = tk
                                last = tk
                            else:
                                B.op("pe", fn, waits=w)
                    piece_consumed(last)
                for tb in range(4):
                    r0 = t * 512 + tb * 128
                    xi = cst["xs_n"] % NXS; cst["xs_n"] += 1
                    tk_x = B.dma("sp", xst[xi], res_dram[r0:r0 + 128, n * 512:(n + 1) * 512],
                                 waits=[cst["xst_free"][xi]], sig=xst_ld[xi])
                    tk_e = B.op("dve", lambda e, tb=tb, xi=xi: e.tensor_tensor(out=ost[xi], in0=P[tb][:, :], in1=xst[xi],
                                                                               op=ALU.add),
                                waits=[tks[tb], tk_x, cst["ost_free"][xi]], sig=dve_ev)
                    cst["acc_ev"][tb] = tk_e
                    cst["xst_free"][xi] = tk_e
                    cst["ost_free"][xi] = B.dma("sp", dst_dram[r0:r0 + 128, n * 512:(n + 1) * 512], ost[xi], waits=[tk_e],
                                                sig=ost_st[xi])
                if after_n is not None:
                    after_n(n)
            return last

        pending_final = []

        def run_pending_final(n=None):
            if pending_final:
                pending_final.pop(0)()

        prev_down_last = None
        for t in range(c.NTC):
            B.dma("sp", mixT3, mixT_d[:, :, t * 512:(t + 1) * 512].rearrange("c p q -> p c q"), waits=[prev_down_last],
                  sig=mix_ld)
            tk_mix = (mix_ld, mix_ld.n)
            tk_gF = B.dma("sp", gF, g_ffn, waits=[prev_down_last], sig=gF_ld)
            last_wo = acc_phase(
                t, (lambda k0, kc, n: w_o[k0 * 128:(k0 + kc) * 128, n * 512:(n + 1) * 512].rearrange("(k p) f -> p k f", p=128)),
                NCH, (lambda ch, tb: mixT3[:, ch, tb * 128:(tb + 1) * 128]), [tk_mix], x_own, x1_d,
                after_n=run_pending_final)
            while pending_final:
                run_pending_final()
            x1_done = [cst["ost_free"][i] for i in range(NXS)]
            tk_h2 = None
            for tb in range(4):
                r0 = t * 512 + tb * 128
                tk_h2 = norm_transpose(xsC, hnC, gF, x1_d[r0:r0 + 128, :],
                                       (lambda c0, n, tb=tb: h2T4[:, c0:c0 + n, tb, :]),
                                       extra_waits=x1_done + [cst["fin_tk"], last_wo], g_wait=tk_gF)
            last_gu = None
            for f0 in range(0, c.DFF, c.PC):
                pg, tkg = req_piece(w_gate[:, f0:f0 + c.PC].rearrange("(c p) f -> p c f", p=128), (NCH, c.PC))
                pu, tku = req_piece(w_up[:, f0:f0 + c.PC].rearrange("(c p) f -> p c f", p=128), (NCH, c.PC))
                lastm = None
                for j in range(c.PC // 128):
                    fi = f0 // 128 + j
                    gi = cst["gu_n"]; cst["gu_n"] += 1
                    bg = P[(gi % 2) * 2]; bu = P[(gi % 2) * 2 + 1]
                    w0 = [tkg, tku, tk_h2, cst["gu_ev"].get(gi - 2)]
                    for (pv, bank, issig) in ((pg, bg, False), (pu, bu, True)):
                        for ch in range(NCH):
                            fn = (lambda e, pv=pv, bank=bank, ch=ch, j=j: e.matmul(
                                bank[:, :], lhsT=pv[:, ch, j * 128:(j + 1) * 128], rhs=h2T[:, ch * 512:(ch + 1) * 512],
                                start=(ch == 0), stop=(ch == NCH - 1)))
                            if issig and ch == NCH - 1:
                                lastm = B.op("pe", fn, sig=pe_gu)
                            else:
                                B.op("pe", fn, waits=w0 if ch == 0 else ())
                    si = gi % 2
                    tk_s = B.op("act", lambda e, bg=bg, si=si: e.activation(out=sgs[si], in_=bg[:, :], func=AF.Silu),
                                waits=[lastm, cst["sg_free"][si]], sig=act_silu)
                    tk_a = B.op("dve", lambda e, bu=bu, si=si, fi=fi: e.tensor_tensor(out=actT3[:, fi, :], in0=sgs[si],
                                                                                      in1=bu[:, :], op=ALU.mult),
                                waits=[tk_s], sig=dve_act)
                    cst["sg_free"][si] = tk_a
                    cst["gu_ev"][gi] = tk_a
                piece_consumed(lastm)
                piece_consumed(lastm)
                last_gu = lastm
            tk_act_done = cst["gu_ev"][cst["gu_n"] - 1]
            last_dn = acc_phase(
                t, (lambda k0, kc, n: w_down[k0 * 128:(k0 + kc) * 128, n * 512:(n + 1) * 512].rearrange("(k p) f -> p k f", p=128)),
                NFC, (lambda ch, tb: actT3[:, ch, tb * 128:(tb + 1) * 128]), [tk_act_done], x1_d, out_d)
            prev_down_last = last_dn
            x2_done = [cst["ost_free"][i] for i in range(NXS)]
            def final_block(tb, t=t, x2_done=x2_done, last_gu=last_gu):
                if tb == 0:
                    cst["gfin_tk"] = B.dma("act", gFin, g_fin, waits=[last_gu], sig=gfin_ld)
                tk_gfin = cst["gfin_tk"]
                r0 = t * 512 + tb * 128
                tk_r = B.dma("act", xrow, out_d[r0:r0 + 128, :], waits=x2_done + [cst["xrow_free"], last_gu], sig=xrow_ld)
                B.op("dve", lambda e: e.memset(ssb[:, 3:4], 0.0), waits=[tk_r])
                for q8 in range(D // 512):
                    B.op("dve", lambda e, q8=q8: e.scalar_tensor_tensor(
                        out=junkF, in0=xrow[:, q8 * 512:(q8 + 1) * 512], scalar=1.0, in1=xrow[:, q8 * 512:(q8 + 1) * 512],
                        op0=ALU.mult, op1=ALU.mult, accum_out=rlb[:, q8 % 8:q8 % 8 + 1]))
                    B.op("dve", lambda e, q8=q8: e.tensor_tensor(out=ssb[:, 3:4], in0=ssb[:, 3:4],
                                                                 in1=rlb[:, q8 % 8:q8 % 8 + 1], op=ALU.add))
                tk_ss = B.op("dve", lambda e: e.tensor_copy(out=ssb[:, 3:4], in_=ssb[:, 3:4]), sig=dve_ss)
                tk_rs = rstd_act(ssb[:, 3:4], rsb[:, 3:4], 1.0 / D, tk_ss)
                tk_fn = B.op("dve", lambda e: e.scalar_tensor_tensor(out=xrow, in0=xrow, scalar=rsb[:, 3:4], in1=gFin,
                                                                     op0=ALU.mult, op1=ALU.mult),
                             waits=[tk_gfin, tk_rs], sig=dve_fin)
                cst["xrow_free"] = B.dma("act", out_d[r0:r0 + 128, :], xrow, waits=[tk_fn], sig=xrow_st)
                cst["fin_tk"] = cst["xrow_free"]

            for tb in range(4):
                pending_final.append(lambda tb=tb, fb=final_block: fb(tb))

        while pending_final:
            run_pending_final()

        B.barrier()
        B.finish()
    return nc


def _bucket(n):
    max_exact = NUM_BUCKETS // 2
    nf = np.maximum(n, 1).astype(np.float32)
    large = max_exact + (np.log(nf / np.float32(max_exact)) / np.float32(math.log(MAX_DISTANCE / max_exact))
                         * np.float32(NUM_BUCKETS - max_exact)).astype(np.int32)
    large = np.minimum(large, NUM_BUCKETS - 1)
    return np.where(n < max_exact, n, large)


def _bucket_jax(n):
    import jax
    import jax.numpy as jnp
    with jax.default_device(jax.devices("cpu")[0]):
        max_exact = NUM_BUCKETS // 2
        nn = jnp.asarray(n, dtype=jnp.int32)
        nf = jnp.maximum(nn, 1).astype(jnp.float32)
        large = max_exact + (jnp.log(nf / max_exact) / math.log(MAX_DISTANCE / max_exact)
                             * (NUM_BUCKETS - max_exact)).astype(jnp.int32)
        large = jnp.minimum(large, NUM_BUCKETS - 1)
        return np.asarray(jnp.where(nn < max_exact, nn, large))


def core_consts(r):
    p = np.arange(128)
    glob = np.where(p < 64, 64 * r + p, 64 * (1 - r) + (p - 64))
    tri = (glob[:, None] <= glob[None, :]).astype(np.float32)
    ohm = np.zeros((128, NUM_BUCKETS + 1, 128), np.float32)
    qq = np.arange(64)
    try:
        bfun = _bucket_jax
        bfun(np.arange(4))
    except Exception:
        bfun = _bucket
    for u in range(2):
        dist = 128 * u + (64 * r + qq)[None, :] - glob[:, None]
        bk = bfun(np.maximum(dist, 0))
        for b in range(NUM_BUCKETS):
            ohm[:, b, u * 64:(u + 1) * 64] = ((bk == b) & (dist >= 0)).astype(np.float32)
        ohm[:, NUM_BUCKETS, u * 64:(u + 1) * 64] = np.where(dist >= 0, 0.0, NEG)
    return tri, ohm.reshape(128, -1)


def make_in_maps(cfg, inputs):
    c = cfg
    x = np.asarray(inputs["x"], np.float32)
    nb = x.shape[0]
    f = lambda k: np.ascontiguousarray(np.asarray(inputs[k], np.float32))
    w_in = f("w_in")[0]; w_o = f("w_o")[0]; w_gate = f("w_gate")[0]; w_up = f("w_up")[0]; w_down = f("w_down")[0]
    bc = lambda v: np.ascontiguousarray(np.broadcast_to(np.asarray(v, np.float32).reshape(1, -1), (128, v.size)))
    g_attn = bc(f("attn_norm_g")[0]); g_ffn = bc(f("ffn_norm_g")[0]); g_fin = bc(f("final_norm_g"))
    g_sub = bc(f("diff_subln_g")[0]); bf_bc = bc(f("b_f")[0])
    lam = np.concatenate([f("lambda_q1")[0], f("lambda_k1")[0], f("lambda_q2")[0], f("lambda_k2")[0]])
    lam_bc = bc(lam)
    tab_bc = bc(f("rel_bias_table").reshape(-1))
    sel = np.zeros((128, c.NFH, 128), np.float32)
    for h in range(c.NFH):
        sel[h, h, :] = 1.0
    sel = sel.reshape(128, -1)
    ident = np.eye(128, dtype=np.float32)
    consts = [core_consts(0), core_consts(1)]
    in_maps = []
    perms = []
    for core in range(2 * nb):
        b, r = core // 2, core % 2
        xb = x[b].reshape(c.NB, 2, 64, c.D)
        x_loc = np.ascontiguousarray(np.stack([xb[:, r], xb[:, 1 - r]], axis=1).reshape(c.S, c.D))
        x_own = np.ascontiguousarray(xb[:, r].reshape(c.NOWN, c.D))
        tri, ohm = consts[r]
        in_maps.append(dict(x_loc=x_loc, x_own=x_own, w_in=w_in, w_o=w_o, w_gate=w_gate, w_up=w_up, w_down=w_down,
                            g_attn=g_attn, g_ffn=g_ffn, g_fin=g_fin, g_sub=g_sub, bf_bc=bf_bc, lam_bc=lam_bc,
                            tab_bc=tab_bc, tri=tri, ohm=ohm, sel=sel, ident=ident))
    return in_maps


def assemble(cfg, results, nb):
    c = cfg
    out = np.empty((nb, c.NB, 2, 64, c.D), np.float32)
    for core in range(2 * nb):
        b, r = core // 2, core % 2
        out[b, :, r] = np.asarray(results[core]["out"]).reshape(c.NB, 64, c.D)
    return out.reshape(nb, c.S, c.D)


_CACHE = {}


def kernel(**inputs):
    cfg = Cfg()
    if "nc" not in _CACHE:
        _CACHE["nc"] = build(cfg)
    nc = _CACHE["nc"]
    in_maps = make_in_maps(cfg, inputs)
    nb = np.asarray(inputs["x"]).shape[0]
    res = run_bass_kernel_spmd(nc, in_maps, core_ids=list(range(2 * nb)))
    return assemble(cfg, res.results, nb)
```

```python
import math
import os
from contextlib import ExitStack

import numpy as np
import concourse.bass as bass
import concourse.mybir as mybir
from concourse.bass_utils import run_bass_kernel_spmd

F32 = mybir.dt.float32
BF16 = mybir.dt.bfloat16
AF = mybir.ActivationFunctionType
ALU = mybir.AluOpType
AX = mybir.AxisListType

EPS = 1e-6
NUM_BUCKETS = 32
MAX_DISTANCE = 128
NEG = -30000.0
LAM_INIT = 0.8 - 0.6 * math.exp(-0.3 * 0)


class Cfg:
    def __init__(s, D=4096, S=4096, DFF=11008):
        s.D = D; s.S = S; s.DFF = DFF
        s.NDH = D // 512; s.NFH = D // 256
        s.DW = s.NDH * 256; s.FW = s.NFH * 128
        s.INC = 3 * s.DW + 3 * s.FW + s.NFH
        s.NCH = D // 128; s.NB = S // 128; s.NOWN = S // 2
        s.NFC = DFF // 128
        s.NHM = 2 * s.NDH + s.NFH
        s.NBA = 8
        s.NTA = S // (128 * s.NBA)
        s.NTC = s.NOWN // 512
        s.NG = s.NOWN // 512
        s.PC = 256
        s.KC = min(16, s.NCH)
        s.SCALE = 128 ** -0.5


class Sem:
    pass


class Builder:
    ENG = ("pe", "act", "dve", "pool", "sp")

    def __init__(self, nc, es):
        self.nc = nc; self.es = es
        self.q = {e: [] for e in self.ENG}
        self.sems = []
        self.waited = {e: {} for e in self.ENG}
        self.seq = {}

    def sem(self, name):
        s = Sem(); s.h = self.es.enter_context(self.nc.semaphore(name)); s.n = 0; s.name = name
        self.sems.append(s)
        return s

    def _ws(self, eng, waits):
        ws = []
        wd = self.waited[eng]
        for w in waits:
            if w is None:
                continue
            s, t = w
            if t is None or t <= 0:
                continue
            if wd.get(s.name, 0) >= t:
                continue
            wd[s.name] = t
            ws.append((s.h, t))
        return ws

    def op(self, eng, fn, waits=(), sig=None, step=1, indep=False, is_dma=False):
        if eng in ("act", "dve") and not is_dma:
            if eng not in self.seq:
                self.seq[eng] = self.sem("seq_" + eng)
            sq = self.seq[eng]
            waits = list(waits)
            if sq.n > 0 and not (indep and not os.environ.get('NOINDEP')):
                waits.append((sq, sq.n))
            sig = sq
            step = 1
        ws = self._ws(eng, waits)
        ticket = None
        sh = None
        if sig is not None:
            sig.n += step; ticket = sig.n; sh = sig.h

        def run(e):
            for h, t in ws:
                e.wait_ge(h, t)
            ins = fn(e)
            if sh is not None:
                ins.then_inc(sh, step)
        self.q[eng].append(run)
        return (sig, ticket) if sig is not None else None

    def dma(self, eng, out, in_, waits=(), sig=None):
        return self.op(eng, lambda e: e.dma_start(out=out, in_=in_), waits, sig, step=16, is_dma=True)

    def wait(self, eng, waits):
        ws = self._ws(eng, waits)
        if ws:
            def run(e):
                for h, t in ws:
                    e.wait_ge(h, t)
            self.q[eng].append(run)

    def barrier(self):
        for eng in self.ENG:
            self.wait(eng, [(s, s.n) for s in self.sems if s.n > 0])

    def finish(self):
        nc = self.nc
        with nc.Block() as block:
            q = self.q

            @block.tensor
            def _(e):
                for f in q["pe"]:
                    f(e)

            @block.scalar
            def _(e):
                for f in q["act"]:
                    f(e)

            @block.vector
            def _(e):
                for f in q["dve"]:
                    f(e)

            @block.gpsimd
            def _(e):
                for f in q["pool"]:
                    f(e)

            @block.sync
            def _(e):
                for f in q["sp"]:
                    f(e)


def build(cfg, debug=False):
    c = cfg
    D, S, NCH, NB, NOWN = c.D, c.S, c.NCH, c.NB, c.NOWN
    NDH, NFH, NHM, NFC = c.NDH, c.NFH, c.NHM, c.NFC
    nc = bass.Bass("TRN2", target_bir_lowering=False)

    def din(name, shape, dt=F32):
        return nc.dram_tensor(name, list(shape), dt, kind="ExternalInput").ap()

    x_loc = din("x_loc", [S, D])
    x_own = din("x_own", [NOWN, D])
    w_in = din("w_in", [D, c.INC])
    w_o = din("w_o", [D, D])
    w_gate = din("w_gate", [D, c.DFF])
    w_up = din("w_up", [D, c.DFF])
    w_down = din("w_down", [c.DFF, D])
    g_attn = din("g_attn", [128, D])
    g_ffn = din("g_ffn", [128, D])
    g_fin = din("g_fin", [128, D])
    g_sub = din("g_sub", [128, 256])
    bf_bc = din("bf_bc", [128, NFH])
    lam_bc = din("lam_bc", [128, 4 * 128])
    tab_bc = din("tab_bc", [128, NUM_BUCKETS * NDH])
    tri_in = din("tri", [128, 128])
    ohm_in = din("ohm", [128, (NUM_BUCKETS + 1) * 128])
    sel_in = din("sel", [128, NFH * 128])
    ident_in = din("ident", [128, 128])

    okind = "ExternalOutput"
    out_d = nc.dram_tensor("out", [NOWN, D], F32, kind=okind).ap()
    skind = "ExternalOutput" if debug else "Internal"
    VC = c.DW + c.FW
    KT_d = nc.dram_tensor("KT_d", [NHM, 128, S], BF16, kind=skind).ap()
    QT_d = nc.dram_tensor("QT_d", [NHM, 128, NOWN], BF16, kind=skind).ap()
    V_d = nc.dram_tensor("V_d", [S, VC], BF16, kind=skind).ap()
    mixT_d = nc.dram_tensor("mixT_d", [NCH, 128, NOWN], BF16, kind=skind).ap()
    x1_d = nc.dram_tensor("x1_d", [NOWN, D], F32, kind=skind).ap()
    cum_d = nc.dram_tensor("cum_d", [128, NB * NFH], F32, kind=skind).ap() if debug else None

    es = ExitStack()
    with es:
        B = Builder(nc, es)

        def sb(name, shape, dt):
            return es.enter_context(nc.sbuf_tensor(name, list(shape), dt))

        NS = 4
        RING_E = 8192
        ring = [sb(f"ring{i}", [128, RING_E], BF16) for i in range(NS)]
        BIG_E = 70656
        big = sb("big", [128, BIG_E], BF16)
        ident = sb("ident_sb", [128, 128], BF16)
        smallf = sb("smallf", [128, 64], F32)
        zero_c = smallf[:, 0:1]
        eps_c = smallf[:, 1:2]
        neglam = smallf[:, 2:3]
        ssb = smallf[:, 4:8]
        rsb = smallf[:, 8:12]
        lsc = smallf[:, 12:20]
        rlb = smallf[:, 20:28]
        nlb = smallf[:, 28:36]

        class Carver:
            def __init__(self, base=0):
                self.off = base

            def take(self, nbytes, dt, shape=None):
                nbytes = (nbytes + 3) // 4 * 4
                assert self.off % 4 == 0
                ne = nbytes // 2
                a = big[:, self.off // 2: self.off // 2 + ne]
                self.off += nbytes
                assert self.off <= BIG_E * 2, self.off
                if dt == F32:
                    a = a.bitcast(F32)
                return a

        TOP_BASE = BIG_E * 2 - 18432
        top = Carver(base=TOP_BASE)
        cumL = top.take(NB * NFH * 4, F32)
        gqT = top.take(NOWN * 2, BF16)
        sel = top.take(NFH * 128 * 2, BF16)
        tri = top.take(128 * 4, F32)
        ones_f = top.take(128 * 4, F32)
        ident_f = top.take(128 * 4, F32)
        tab = top.take(NUM_BUCKETS * NDH * 4, F32)
        gsub = top.take(256 * 4, F32)
        bfb = top.take(NFH * 4, F32)
        runL = top.take(NFH * 4, F32)
        Wf_sb = top.take(NCH * NFH * 2, BF16)
        lamv = top.take(4 * 128 * 4, F32)
        assert top.off <= BIG_E * 2

        P = [es.enter_context(nc.psum_tensor(f"P{i}", [128, 512], F32)) for i in range(6)]
        PT0 = es.enter_context(nc.psum_tensor("PT0", [128, 1024], BF16))
        P6 = es.enter_context(nc.psum_tensor("P6", [128, 512], F32))
        PT = [PT0[:, :], P6[:, :].bitcast(BF16)]
        SB = [P[0], P[1], P6]
        NSB = 3

        ring_ld = [B.sem(f"ring_ld{i}") for i in range(NS)]
        pe_piece = B.sem("pe_piece")
        st = {"piece": 0, "piece_tk": []}

        piece_srcs = []

        def req_piece(src_ap, shape3):
            i = st["piece"]; st["piece"] += 1
            s = i % NS
            a, b = shape3
            dst = ring[s][:, 0:a * b].rearrange("p (a b) -> p a b", a=a)
            waits = []
            if i >= NS:
                waits.append(st["piece_tk"][i - NS])
            tk = B.dma("pool", dst, src_ap, waits=waits, sig=ring_ld[s])
            return dst, tk

        def piece_done_sig():
            return pe_piece

        def piece_consumed(tk):
            st["piece_tk"].append(tk)


        setup_ld = B.sem("setup_ld")
        tks = []
        tks.append(B.dma("sp", tri, tri_in, sig=setup_ld))
        tks.append(B.dma("sp", ident_f, ident_in, sig=setup_ld))
        tks.append(B.dma("sp", tab, tab_bc, sig=setup_ld))
        tks.append(B.dma("sp", gsub, g_sub, sig=setup_ld))
        tks.append(B.dma("sp", bfb, bf_bc, sig=setup_ld))
        tks.append(B.dma("sp", lamv, lam_bc, sig=setup_ld))
        setup_c = B.sem("setup_c")
        tks.append(B.dma("pool", ident[:], ident_in, sig=setup_c))
        tks.append(B.dma("pool", sel[:, :], sel_in, sig=setup_c))
        wf_src = w_in[:, c.INC - NFH: c.INC].rearrange("(c p) f -> p c f", p=128)
        tks.append(B.dma("pool", Wf_sb.rearrange("p (c f) -> p c f", c=NCH), wf_src, sig=setup_c))
        setup_tk = [tks[5], tks[8]]

        dve_setup = B.sem("dve_setup")
        act_setup = B.sem("act_setup")
        B.op("dve", lambda e: e.memset(smallf[:, :], 0.0), waits=setup_tk)
        B.op("dve", lambda e: e.memset(eps_c, EPS))
        B.op("dve", lambda e: e.memset(ones_f, 1.0))
        B.op("dve", lambda e: e.memset(runL, 0.0))
        B.op("dve", lambda e: e.memset(gqT, 0.0))
        B.op("dve", lambda e: e.tensor_scalar(out=gsub, in0=gsub, scalar1=1.0 - LAM_INIT, scalar2=None, op0=ALU.mult))
        lv = lamv.rearrange("p (a d) -> p a d", a=4)
        junk128 = top.take(128 * 4, F32)
        B.op("dve", lambda e: e.scalar_tensor_tensor(out=junk128, in0=lv[:, 0, :], scalar=1.0, in1=lv[:, 1, :],
                                                     op0=ALU.mult, op1=ALU.mult, accum_out=lsc[:, 0:1]))
        tk_l = B.op("dve", lambda e: e.scalar_tensor_tensor(out=junk128, in0=lv[:, 2, :], scalar=1.0, in1=lv[:, 3, :],
                                                            op0=ALU.mult, op1=ALU.mult, accum_out=lsc[:, 1:2]),
                    sig=dve_setup)
        tk_e = B.op("act", lambda e: e.activation(out=lsc[:, 2:4], in_=lsc[:, 0:2], func=AF.Exp), waits=[tk_l],
                    sig=act_setup)
        B.op("dve", lambda e: e.tensor_tensor(out=lsc[:, 4:5], in0=lsc[:, 3:4], in1=lsc[:, 2:3], op=ALU.subtract),
             waits=[tk_e])
        tk_setup_dve = B.op("dve", lambda e: e.tensor_scalar(out=neglam, in0=lsc[:, 4:5], scalar1=-LAM_INIT, scalar2=None,
                                                             op0=ALU.add), sig=dve_setup)

        dve_ss = B.sem("dve_ss")
        act_rs = B.sem("act_rs")

        def rstd_act(ss_col, rs_col, inv_n, tk_ss):
            B.op("act", lambda e: e.activation(out=ss_col, in_=ss_col, func=AF.Ln, bias=eps_c, scale=inv_n), waits=[tk_ss])
            return B.op("act", lambda e: e.activation(out=rs_col, in_=ss_col, func=AF.Exp, scale=-0.5), sig=act_rs)

        xs_ld = [B.sem("xs_ld0"), B.sem("xs_ld1")]
        dve_norm = B.sem("dve_norm")
        pe_tp = B.sem("pe_tp")
        act_tp = B.sem("act_tp")
        nst = {"n": 0, "norm_tk": [], "tp_last": [], "tpg": 0, "tp_ev": []}

        def norm_transpose(xs, hn, g_tile, src_rows, hT_dst, extra_waits=(), g_wait=None, mid_hook=None):
            n = nst["n"]; nst["n"] += 1
            b = n % 2
            w = list(extra_waits)
            back = 1 if xs[0] is xs[1] else 2
            if n >= back:
                w.append(nst["norm_tk"][n - back])
            tk_ld = B.dma("sp", xs[b], src_rows, waits=w, sig=xs_ld[b])
            w = [tk_ld, tk_setup_dve]
            if n >= 2:
                w.append(nst["tp_last"][n - 2])
            tk_ss = B.op("dve", lambda e: e.scalar_tensor_tensor(out=hn[b], in0=xs[b], scalar=1.0, in1=xs[b], op0=ALU.mult,
                                                                 op1=ALU.mult, accum_out=ssb[:, b:b + 1]), waits=w, sig=dve_ss)
            tk_rs = rstd_act(ssb[:, b:b + 1], rsb[:, b:b + 1], 1.0 / D, tk_ss)
            tk_n = B.op("dve", lambda e: e.scalar_tensor_tensor(out=hn[b], in0=xs[b], scalar=rsb[:, b:b + 1], in1=g_tile,
                                                                op0=ALU.mult, op1=ALU.mult),
                        waits=[g_wait, tk_rs], sig=dve_norm)
            nst["norm_tk"].append(tk_n)
            if mid_hook is not None:
                mid_hook()
            last = None
            ev = None
            for c0 in range(0, NCH, 8):
                ng = min(8, NCH - c0)
                gi = nst["tpg"]; nst["tpg"] += 1
                pb = PT[gi % 2]
                w = [tk_n]
                if gi >= 2:
                    w.append(nst["tp_ev"][gi - 2])
                for k in range(ng):
                    cc = c0 + k
                    is_last = (k == ng - 1)
                    fn = (lambda e, cc=cc, k=k, pb=pb: e.transpose(out=pb[:, k * 128:(k + 1) * 128],
                                                                    in_=hn[b][:, cc * 128:(cc + 1) * 128], identity=ident[:]))
                    if is_last:
                        last = B.op("pe", fn, waits=w if k == 0 else (), sig=pe_tp)
                    else:
                        B.op("pe", fn, waits=w if k == 0 else ())
                dst = hT_dst(c0, ng)
                src = pb[:, 0:ng * 128].rearrange("p (a b) -> p a b", a=ng)
                ev = B.op("act", lambda e, dst=dst, src=src: e.copy(out=dst, in_=src), waits=[last], sig=act_tp)
                nst["tp_ev"].append(ev)
            nst["tp_last"].append(last)
            return ev

        ca = Carver()
        NBA = c.NBA
        TA = NBA * 128
        hT = ca.take(NCH * TA * 2, BF16)
        hT4 = hT.rearrange("p (c t q) -> p c t q", c=NCH, t=NBA)
        xsA0 = ca.take(D * 4, F32)
        xsA = [xsA0, xsA0]
        hnA = [ca.take(D * 2, BF16) for _ in range(2)]
        gA = ca.take(D * 4, F32)
        NSTG = 3
        stgK = [ca.take(512 * 2, BF16) for _ in range(NSTG)]
        zf = ca.take(NFH * 4, F32)
        ef = ca.take(NFH * 4, F32)
        lf = ca.take(NFH * 4, F32)
        assert ca.off <= TOP_BASE, ca.off

        gA_ld = B.sem("gA_ld")
        tk_gA = B.dma("sp", gA, g_attn, sig=gA_ld)

        pe_proj = B.sem("pe_proj")
        act_ev = B.sem("act_ev")
        stg_st = [B.sem(f"stg_st{i}") for i in range(NSTG)]
        pe_f = B.sem("pe_f")
        dve_f = B.sem("dve_f")
        act_f = B.sem("act_f")
        pe_c = B.sem("pe_c")
        dve_c = B.sem("dve_c")
        pst = {"grp": 0, "ev_tk": [], "stg_n": 0, "stg_tk": [None] * NSTG, "last_mm": None}

        def proj_group(mm_list, ncols, dst_dram, split=None, defer=None):
            gi = pst["grp"]; pst["grp"] += 1
            bank = P[gi % 3]
            w = []
            if gi >= 3:
                w.append(pst["ev_tk"][gi - 3])
            nmm = len(mm_list)
            tk = None
            for k, (lhsT, rhs, extra_w) in enumerate(mm_list):
                ww = list(w) if k == 0 else []
                ww += list(extra_w)
                ov = bank[:, 0:ncols] if split is None else bank[:, 0:ncols].rearrange("p (a b) -> p a b", a=split)
                fn = (lambda e, lhsT=lhsT, rhs=rhs, k=k, ov=ov: e.matmul(ov, lhsT=lhsT, rhs=rhs,
                                                                         start=(k == 0), stop=(k == nmm - 1)))
                if k == nmm - 1:
                    tk = B.op("pe", fn, waits=ww, sig=pe_proj)
                else:
                    B.op("pe", fn, waits=ww)
            def emit_evac():
                assert len(pst["ev_tk"]) == gi
                si = pst["stg_n"] % NSTG; pst["stg_n"] += 1
                stg = stgK[si][:, 0:ncols]
                tk_ev = B.op("act", lambda e: e.copy(out=stg, in_=bank[:, 0:ncols]), waits=[tk, pst["stg_tk"][si]],
                             sig=act_ev)
                pst["ev_tk"].append(tk_ev)
                pst["stg_tk"][si] = B.dma("sp", dst_dram, stg, waits=[tk_ev], sig=stg_st[si])

            if defer is None:
                emit_evac()
            else:
                defer.append(emit_evac)
            return tk

        def w_in_piece(col0):
            src = w_in[:, col0:col0 + c.PC].rearrange("(c p) f -> p c f", p=128)
            return req_piece(src, (NCH, c.PC))

        C_DQ, C_DK, C_DV = 0, c.DW, 2 * c.DW
        C_FQ, C_FK, C_FV = 3 * c.DW, 3 * c.DW + c.FW, 3 * c.DW + 2 * c.FW
        Wf3 = Wf_sb.rearrange("p (c f) -> p c f", c=NCH)
        cumL3 = cumL.rearrange("p (b h) -> p b h", b=NB)

        hT_free = None
        for t in range(c.NTA):
            tk_hT = None
            v_pieces = [(cb, vc0, pc0) for (cb, vc0, width) in ((C_DV, 0, c.DW), (C_FV, c.DW, c.FW))
                        for pc0 in range(0, width, c.PC)]
            NV1 = min(3, len(v_pieces))
            early = []
            early_last = [None] * NV1
            deferred = []

            def flush_deferred():
                while deferred:
                    deferred.pop(0)()

            for tb in range(NBA):
                blk = t * NBA + tb
                tk_blk = norm_transpose(
                    xsA, hnA, gA, x_loc[blk * 128:(blk + 1) * 128, :],
                    (lambda c0, n, tb=tb: hT4[:, c0:c0 + n, tb, :]),
                    extra_waits=[hT_free] if hT_free is not None else [], g_wait=tk_gA, mid_hook=flush_deferred)
                tk_hT = tk_blk
                if tb == 0:
                    for (cb, vc0, pc0) in v_pieces[:NV1]:
                        pv, tkp = w_in_piece(cb + pc0)
                        early.append((pv, tkp, vc0 + pc0))
                r0 = t * TA + tb * 128
                for ei, (pv, tkp, vcol) in enumerate(early):
                    mm = [(hT4[:, cc, tb, :], pv[:, cc, :], [tkp, tk_blk] if cc == 0 else []) for cc in range(NCH)]
                    early_last[ei] = proj_group(mm, c.PC, V_d[r0:r0 + 128, vcol: vcol + c.PC], defer=deferred)
            flush_deferred()
            for ei in range(NV1):
                piece_consumed(early_last[ei])
            hw = [tk_hT]
            for tb in range(NBA):
                blk = t * NBA + tb
                for cc in range(NCH):
                    fn = (lambda e, cc=cc, tb=tb: e.matmul(P[3][:, 0:NFH], lhsT=hT4[:, cc, tb, :], rhs=Wf3[:, cc, :],
                                                            start=(cc == 0), stop=(cc == NCH - 1)))
                    if cc == NCH - 1:
                        tk_f = B.op("pe", fn, sig=pe_f)
                    else:
                        w = list(hw) if cc == 0 else []
                        if cc == 0 and blk > 0:
                            w.append(pst["zf_tk"])
                        B.op("pe", fn, waits=w)
                tkz = B.op("dve", lambda e: e.tensor_tensor(out=zf, in0=P[3][:, 0:NFH], in1=bfb, op=ALU.add),
                           waits=[tk_f, pst.get("lf_free")], sig=dve_f)
                pst["zf_tk"] = tkz
                B.op("act", lambda e: e.activation(out=ef, in_=zf, func=AF.Exp, scale=-1.0), waits=[tkz, pst.get("c_tk")])
                tkl = B.op("act", lambda e: e.activation(out=lf, in_=ef, func=AF.Ln, bias=1.0, scale=1.0), sig=act_f)
                B.op("pe", lambda e: e.matmul(P[4][:, 0:NFH], lhsT=tri, rhs=lf, start=True, stop=True),
                     waits=[tkl, pst.get("cum_ev")])
                tkc = B.op("pe", lambda e: e.matmul(P[4][:, 64:64 + NFH], lhsT=ones_f, rhs=lf, start=True, stop=True),
                           sig=pe_c)
                pst["c_tk"] = tkc
                B.op("dve", lambda e, blk=blk: e.tensor_tensor(out=cumL3[:, blk, :], in0=P[4][:, 0:NFH], in1=runL,
                                                               op=ALU.add), waits=[tkc])
                tkr = B.op("dve", lambda e: e.tensor_tensor(out=runL, in0=P[4][:, 64:64 + NFH], in1=runL, op=ALU.add),
                           sig=dve_c)
                pst["lf_free"] = tkc
                tkg = B.op("pe", lambda e, blk=blk: e.matmul(P[5][0:NFH, 0:128], lhsT=cumL3[:, blk, :], rhs=ident_f,
                                                              start=True, stop=True),
                           waits=[tkr, pst.get("gq_ev")], sig=pe_c)
                tkge = B.op("dve", lambda e, blk=blk: e.tensor_scalar(out=gqT[0:NFH, blk * 64:(blk + 1) * 64],
                                                                       in0=P[5][0:NFH, 0:64], scalar1=-1.0 / c.SCALE,
                                                                       scalar2=None, op0=ALU.mult),
                            waits=[tkg], sig=dve_c)
                pst["cum_ev"] = tkr
                pst["gq_ev"] = tkge
            last_tk = None
            for (cb, hm0) in ((C_DQ, 0), (C_FQ, 2 * NDH)):
                ncolblk = (c.DW if cb == C_DQ else c.FW) // 128
                for pc0 in range(0, ncolblk * 128, c.PC):
                    pv, tkp = w_in_piece(cb + pc0)
                    for j in range(c.PC // 128):
                        hm = hm0 + (pc0 // 128) + j
                        for q0 in range(0, NBA, 8):
                            nq = min(8, NBA - q0)
                            mm = [(pv[:, cc, j * 128:(j + 1) * 128], hT4[:, cc, q0:q0 + nq, 0:64],
                                   ([tkp] + hw) if cc == 0 else []) for cc in range(NCH)]
                            o0 = t * NBA * 64 + q0 * 64
                            last_tk = proj_group(mm, nq * 64, QT_d[hm, :, o0:o0 + nq * 64], split=nq)
                    piece_consumed(last_tk)
            for (cb, hm0) in ((C_DK, 0), (C_FK, 2 * NDH)):
                ncolblk = (c.DW if cb == C_DK else c.FW) // 128
                for pc0 in range(0, ncolblk * 128, c.PC):
                    pv, tkp = w_in_piece(cb + pc0)
                    for j in range(c.PC // 128):
                        hm = hm0 + (pc0 // 128) + j
                        for k0 in range(0, NBA, 4):
                            mm = [(pv[:, cc, j * 128:(j + 1) * 128], hT4[:, cc, k0:k0 + 4, :],
                                   ([tkp] + hw) if cc == 0 else []) for cc in range(NCH)]
                            o0 = t * TA + k0 * 128
                            last_tk = proj_group(mm, 512, KT_d[hm, :, o0:o0 + 512], split=4)
                    piece_consumed(last_tk)
            for (cb, vc0, pc0) in v_pieces[NV1:]:
                pv, tkp = w_in_piece(cb + pc0)
                for tb in range(NBA):
                    r0 = t * TA + tb * 128
                    mm = [(hT4[:, cc, tb, :], pv[:, cc, :], ([tkp] + hw) if cc == 0 else []) for cc in range(NCH)]
                    last_tk = proj_group(mm, c.PC, V_d[r0:r0 + 128, vc0 + pc0: vc0 + pc0 + c.PC])
                piece_consumed(last_tk)
            hT_free = last_tk

        if debug:
            B.dma("sp", cum_d, cumL, waits=[pst["cum_ev"]], sig=setup_ld)
        B.barrier()

        cb_ = Carver()
        KTs = [cb_.take(S * 2, BF16) for _ in range(2)]
        QTs = [cb_.take(NOWN * 2, BF16) for _ in range(2)]
        VdS = [cb_.take(NB * 257 * 2 + 2, BF16) for _ in range(2)]
        vf_off = cb_.off
        VfS = [cb_.take(NB * 129 * 2 + 2, BF16) for _ in range(2)]
        assert c.NG * 4 * 256 * 4 <= cb_.off - vf_off, "O1 alias does not fit in the fox V buffers"
        O1 = big[:, vf_off // 2: vf_off // 2 + c.NG * 4 * 256 * 2].bitcast(F32)
        Wt = [cb_.take(576 * 4, F32) for _ in range(NDH + 1)]
        NPT = 4
        PTs = [cb_.take(512 * 2, BF16) for _ in range(NPT)]
        tmpS = [cb_.take(512 * 4, F32) for _ in range(3)]
        Of = cb_.take(256 * 4, F32)
        On = [cb_.take(256 * 2, BF16) for _ in range(4)]
        mixS = [cb_.take(2 * 512 * 2, BF16) for _ in range(2)]
        ohm = cb_.take((NUM_BUCKETS + 1) * 128 * 2, BF16)
        assert cb_.off <= TOP_BASE, cb_.off
        O14 = O1.rearrange("p (g a d) -> p g a d", g=c.NG, a=4)

        ohm_ld = B.sem("ohm_ld")
        tk_ohm = B.dma("pool", ohm, ohm_in, sig=ohm_ld)
        ohm3 = ohm.rearrange("p (b q) -> p b q", b=NUM_BUCKETS + 1)
        dve_w = B.sem("dve_w")
        tk_w = None
        for h in range(NDH):
            W = Wt[h]
            B.op("dve", lambda e, W=W: e.tensor_copy(out=W[:, 0:128], in_=ohm3[:, NUM_BUCKETS, :]), waits=[tk_ohm])
            for bk in range(NUM_BUCKETS):
                B.op("dve", lambda e, W=W, bk=bk, h=h: e.scalar_tensor_tensor(
                    out=W[:, 0:128], in0=ohm3[:, bk, :], scalar=tab[:, bk * NDH + h: bk * NDH + h + 1], in1=W[:, 0:128],
                    op0=ALU.mult, op1=ALU.add))
            B.op("dve", lambda e, W=W: e.memset(W[:, 128:576], 0.0))
            tk_w = B.op("dve", lambda e, W=W, h=h: e.tensor_scalar(
                out=W[:, 128:576], in0=W[:, 128:576], scalar1=tab[:, 31 * NDH + h: 31 * NDH + h + 1], scalar2=None,
                op0=ALU.add), sig=dve_w)
        W = Wt[NDH]
        B.op("dve", lambda e: e.memset(W[:, :], 0.0))
        tk_w = B.op("dve", lambda e: e.tensor_copy(out=W[:, 0:64], in_=ohm3[:, NUM_BUCKETS, 0:64]), sig=dve_w)
        for bi in range(2):
            vd3 = VdS[bi][:, 0:NB * 257].rearrange("p (b d) -> p b d", b=NB)
            vf3 = VfS[bi][:, 0:NB * 129].rearrange("p (b d) -> p b d", b=NB)
            tk_w = B.op("dve", lambda e, vd3=vd3: e.memset(vd3[:, :, 256:257], 1.0))

        kq_ld = [B.sem("kq_ld0"), B.sem("kq_ld1")]
        vd_ld = [B.sem("vd_ld0"), B.sem("vd_ld1")]
        vf_ld = [B.sem("vf_ld0"), B.sem("vf_ld1")]
        pe_s = B.sem("pe_s")
        dve_pre = B.sem("dve_pre")
        act_exp = B.sem("act_exp")
        pe_pv = B.sem("pe_pv")
        pe_o = B.sem("pe_o")
        dve_o = B.sem("dve_o")
        pe_ot = B.sem("pe_ot")
        act_ot = B.sem("act_ot")
        mix_st = [B.sem("mix_st0"), B.sem("mix_st1")]

        V_d3 = V_d.rearrange("(b p) v -> p b v", p=128)
        ast = {"n": 0, "s_tk": {}, "exp_tk": {}, "pv_tk": {}, "unit": 0, "kq_free": [None, None],
               "vd_n": 0, "vf_n": 0, "vd_free": [None, None], "vf_free": [None, None], "o_ev": None,
               "on_n": 0, "ot_ev": [None, None, None, None], "ot_n": 0, "mix_n": 0, "mix_free": [None, None], "pre_tk": {}}

        units = []
        for h in range(NDH):
            units.append(("d", h, 0)); units.append(("d", h, 1))
        for h in range(NFH):
            units.append(("f", h, 0))

        def load_unit(ui):
            kind, h, m = units[ui]
            b = ui % 2
            hm = (2 * h + m) if kind == "d" else (2 * NDH + h)
            w = [ast["kq_free"][b]]
            B.dma("sp", KTs[b], KT_d[hm], waits=w, sig=kq_ld[b])
            tk = B.dma("sp", QTs[b], QT_d[hm], sig=kq_ld[b])
            tkv = None
            if kind == "d" and m == 0:
                vb = ast["vd_n"] % 2; ast["vd_n"] += 1
                vd3 = VdS[vb][:, 0:NB * 257].rearrange("p (b d) -> p b d", b=NB)
                tkv = B.dma("sp", vd3[:, :, 0:256], V_d3[:, :, h * 256:(h + 1) * 256],
                            waits=[ast["vd_free"][vb], tk_w], sig=vd_ld[vb])
                ast["cur_vd"] = (vb, tkv)
            elif kind == "f":
                if ast.get("diff_final") is None:
                    ast.setdefault("deferred", []).append(ui)
                else:
                    load_vf(ui)
            return tk

        def load_vf(ui):
            kind, h, m = units[ui]
            vb = ast["vf_n"] % 2; ast["vf_n"] += 1
            vf3 = VfS[vb][:, 0:NB * 129].rearrange("p (b d) -> p b d", b=NB)
            if ast["vf_n"] <= 2:
                tko = B.op("dve", lambda e, vf3=vf3: e.memset(vf3[:, :, 128:129], 1.0), waits=[ast["diff_final"]])
            else:
                tko = None
            tkv = B.dma("sp", vf3[:, :, 0:128], V_d3[:, :, c.DW + h * 128: c.DW + (h + 1) * 128],
                        waits=[ast["vf_free"][vb], tk_w, ast["diff_final"], tko], sig=vf_ld[vb])
            unit_v[ui] = (vb, tkv, tko)

        unit_ld = {}
        unit_v = {}

        def issue_load(ui):
            if ui < len(units) and ui not in unit_ld:
                unit_ld[ui] = load_unit(ui)
                kind, h, m = units[ui]
                if kind == "d":
                    unit_v[ui] = ast["cur_vd"] + (None,)

        issue_load(0)
        issue_load(1)

        for ui, (kind, h, m) in enumerate(units):
            b = ui % 2
            tk_kq = unit_ld[ui]
            vb, tk_v, tk_ones = unit_v[ui]
            O13 = O14[:, 0, :, :]
            KT = KTs[b]; QT = QTs[b]
            if kind == "d":
                V3 = VdS[vb][:, 0:NB * 257].rearrange("p (b d) -> p b d", b=NB)
                NV = 257
                Wm = Wt[h]
            else:
                V3 = VfS[vb][:, 0:NB * 129].rearrange("p (b d) -> p b d", b=NB)
                NV = 129
                Wm = Wt[NDH]
            last_pe = None
            for g in range(c.NG):
                O13 = O14[:, g, :, :]
                nj = 8 * g + 8
                tiles = []
                for j in range(nj):
                    mrel = j - 8 * g
                    lo = 64 * mrel if mrel >= 0 else 0
                    n = 512 - lo
                    if kind == "d":
                        special = (mrel >= -1)
                        woff = 64 if mrel == -1 else 0
                    else:
                        special = (mrel >= 0)
                        woff = 0
                    tiles.append((j, lo, n, special, woff))

                def emit_S(idx):
                    j, lo, n, special, woff = tiles[idx]
                    k = ast["n"] + idx
                    sbk = SB[k % NSB]
                    w = [tk_kq, tk_setup_dve, ast.get(("ptev", 0))]
                    if k >= NSB:
                        w.append(ast["exp_tk"].get(k - NSB))
                        w.append(ast["pre_tk"].get(k - NSB))
                    lhsT = KT[:, j * 128:(j + 1) * 128]
                    rhs = QT[:, g * 512 + lo: (g + 1) * 512]
                    if kind == "d":
                        ast["s_tk"][k] = B.op("pe", lambda e: e.matmul(sbk[:, 0:n], lhsT=lhsT, rhs=rhs, start=True, stop=True),
                                              waits=w, sig=pe_s)
                    else:
                        B.op("pe", lambda e: e.matmul(sbk[:, 0:n], lhsT=lhsT, rhs=rhs, start=True, stop=False), waits=w)
                        lhs2 = sel[:, h * 128:(h + 1) * 128]
                        rhs2 = gqT[:, g * 512 + lo:(g + 1) * 512]
                        ast["s_tk"][k] = B.op("pe", lambda e: e.matmul(sbk[:, 0:n], lhsT=lhs2, rhs=rhs2, start=False,
                                                                        stop=True), sig=pe_s)

                def emit_exp(idx):
                    j, lo, n, special, woff = tiles[idx]
                    k = ast["n"] + idx
                    sbk = SB[k % NSB]
                    pt = PTs[k % NPT]
                    wpt = ast["pv_tk"].get(k - NPT)
                    if kind == "d":
                        bias_full = tab[:, 31 * NDH + h: 31 * NDH + h + 1]
                        bias_sp = zero_c
                    else:
                        bias_full = cumL3[:, j, h:h + 1]
                        bias_sp = bias_full
                    if special:
                        if kind == "d":
                            wc = min(n, 128) if woff == 0 else 64
                        else:
                            wc = 64
                        wc = n
                        tm = tmpS[k % 3]
                        wv = Wm[:, woff:woff + wc]
                        tkp = B.op("dve", lambda e: e.scalar_tensor_tensor(
                            out=tm[:, 0:wc], in0=sbk[:, 0:wc], scalar=c.SCALE, in1=wv, op0=ALU.mult,
                            op1=ALU.add), waits=[ast["s_tk"][k], ast["exp_tk"].get(k - 3), tk_w], sig=dve_pre, indep=True)
                        ast["pre_tk"][k] = tkp
                        if n > wc:
                            B.op("act", lambda e: e.activation(out=pt[:, wc:n], in_=sbk[:, wc:n], func=AF.Exp,
                                                               bias=bias_full, scale=c.SCALE),
                                 waits=[ast["s_tk"][k], wpt], sig=act_exp, indep=True)
                        ast["exp_tk"][k] = B.op("act", lambda e: e.activation(out=pt[:, 0:wc], in_=tm[:, 0:wc], func=AF.Exp,
                                                                               bias=bias_sp, scale=1.0),
                                                waits=[tkp, wpt], sig=act_exp, indep=True)
                    else:
                        ast["exp_tk"][k] = B.op("act", lambda e: e.activation(out=pt[:, 0:n], in_=sbk[:, 0:n], func=AF.Exp,
                                                                               bias=bias_full, scale=c.SCALE),
                                                waits=[ast["s_tk"][k], wpt], sig=act_exp, indep=True)

                def emit_pv(idx):
                    j, lo, n, special, woff = tiles[idx]
                    k = ast["n"] + idx
                    pt = PTs[k % NPT]
                    tk = None
                    first = True
                    for qc in range(4):
                        c_lo = max(128 * qc, lo)
                        c_hi = 128 * qc + 128
                        if c_lo >= c_hi:
                            continue
                        r0 = c_lo - 128 * qc
                        i1 = 8 * g + 2 * qc + 1
                        w = [ast["exp_tk"][k], tk_v, tk_ones] if first else []
                        if j == 0:
                            w.append(ast["o_ev"])
                        first = False
                        rv_ = V3[:, j, 0:NV]
                        if j == i1 - 1 and r0 == 0 and not os.environ.get('NOSPLITMM'):
                            ov = P[2 + qc][0:64, 0:NV]
                            lv_ = pt[:, c_lo - lo:c_lo - lo + 64]
                            B.op("pe", lambda e, ov=ov, lv_=lv_, rv_=rv_, st_=(j == 0): e.matmul(
                                ov, lhsT=lv_, rhs=rv_, start=st_, stop=True), waits=w)
                            w = []
                            ov = P[2 + qc][64:128, 0:NV]
                            lv_ = pt[:, c_lo - lo + 64:c_hi - lo]
                            tk = B.op("pe", lambda e, ov=ov, lv_=lv_, rv_=rv_, st_=(j == 0): e.matmul(
                                ov, lhsT=lv_, rhs=rv_, start=st_, stop=False), waits=w, sig=pe_pv)
                        else:
                            ov = P[2 + qc][r0:128, 0:NV]
                            lv_ = pt[:, c_lo - lo:c_hi - lo]
                            tk = B.op("pe", lambda e, ov=ov, lv_=lv_, rv_=rv_, st_=(j == 0), sp_=(j == i1): e.matmul(
                                ov, lhsT=lv_, rhs=rv_, start=st_, stop=sp_), waits=w, sig=pe_pv)
                    ast["pv_tk"][k] = tk
                    return tk

                nt = len(tiles)
                for i_ in range(min(NSB, nt)):
                    emit_S(i_)
                p2a_at = 1
                p2b_at = min(nt - 1, 12)
                for idx in range(nt):
                    emit_exp(idx)
                    if idx + NSB < nt:
                        emit_S(idx + NSB)
                    last_pe = emit_pv(idx)
                    if idx == p2a_at and ast.get("pend_a"):
                        ast["pend_a"](); ast["pend_a"] = None
                    if idx == p2b_at and ast.get("pend_b"):
                        ast["pend_b"](); ast["pend_b"] = None
                ast["n"] += nt
                tk_o = last_pe

                if kind == "d" and m == 0:
                    tkr = []
                    for qc in range(4):
                        tkr.append(B.op("dve", lambda e, qc=qc: e.reciprocal(out=rlb[:, qc:qc + 1], in_=P[2 + qc][:, 256:257]),
                                        waits=[tk_o, ast.get("o1_free")], sig=dve_o, indep=(qc > 0)))
                    for qc in range(4):
                        tk_e = B.op("dve", lambda e, qc=qc, O13=O13: e.tensor_scalar(out=O13[:, qc, :], in0=P[2 + qc][:, 0:256],
                                                                            scalar1=rlb[:, qc:qc + 1], scalar2=None,
                                                                            op0=ALU.mult), waits=[tkr[qc]], sig=dve_o, indep=True)
                    ast["o_ev"] = tk_e
                else:
                    mi = ast["mix_n"] % 2; ast["mix_n"] += 1
                    nchk = 2 if kind == "d" else 1
                    ms3 = mixS[mi].rearrange("p (a q) -> p a q", a=2)
                    ons = []
                    for qc in range(4):
                        oi = ast["on_n"] % 4; ast["on_n"] += 1
                        ons.append(oi)
                    if kind == "d":
                        tkr = []
                        tkn = []
                        for qc in range(4):
                            tkr.append(B.op("dve", lambda e, qc=qc: e.reciprocal(out=rlb[:, qc:qc + 1], in_=P[2 + qc][:, 256:257]),
                                            waits=[tk_o], sig=dve_o, indep=(qc > 0)))
                        for qc in range(4):
                            tkn.append(B.op("dve", lambda e, qc=qc: e.tensor_scalar(out=nlb[:, qc:qc + 1], in0=rlb[:, qc:qc + 1],
                                                                                    scalar1=neglam, scalar2=None, op0=ALU.mult),
                                            waits=[tkr[qc]], sig=dve_o, indep=True))
                        for qc in range(4):
                            tk_e = B.op("dve", lambda e, qc=qc, O13=O13: e.scalar_tensor_tensor(
                                out=O13[:, qc, :], in0=P[2 + qc][:, 0:256], scalar=nlb[:, qc:qc + 1], in1=O13[:, qc, :],
                                op0=ALU.mult, op1=ALU.add), waits=[tkn[qc]], sig=dve_o, indep=True)
                        tk_on_list = [None] * 4
                    else:
                        tk_on_list = []
                        for qc in range(4):
                            on = On[ons[qc]]
                            B.op("dve", lambda e, qc=qc: e.reciprocal(out=rlb[:, qc:qc + 1], in_=P[2 + qc][:, 128:129]),
                                 waits=[tk_o])
                            tk_e = B.op("dve", lambda e, qc=qc, on=on: e.tensor_scalar(
                                out=on[:, 0:128], in0=P[2 + qc][:, 0:128], scalar1=rlb[:, qc:qc + 1], scalar2=None,
                                op0=ALU.mult), waits=[ast["ot_ev"][ons[qc]]], sig=dve_o)
                            tk_on_list.append(tk_e)
                    ast["o_ev"] = tk_e
                    if kind == "d":
                        ch0 = 2 * h
                    else:
                        ch0 = 2 * NDH + h
                    dstm = mixT_d[ch0:ch0 + nchk, :, g * 512:(g + 1) * 512].rearrange("a p q -> p a q")

                    def phase2a(kind=kind, O13=O13, ons=ons, tk_on_list=tk_on_list):
                        if kind != "d":
                            return
                        for qc in range(4):
                            on = On[ons[qc]]
                            tk_s = B.op("dve", lambda e, qc=qc, O13=O13: e.scalar_tensor_tensor(
                                out=Of, in0=O13[:, qc, :], scalar=1.0, in1=O13[:, qc, :], op0=ALU.mult, op1=ALU.mult,
                                accum_out=ssb[:, 2:3]), sig=dve_o)
                            tk_rs = rstd_act(ssb[:, 2:3], rsb[:, 2:3], 1.0 / 256, tk_s)
                            tk_on_list[qc] = B.op("dve", lambda e, qc=qc, on=on, O13=O13: e.scalar_tensor_tensor(
                                out=on[:, 0:256], in0=O13[:, qc, :], scalar=rsb[:, 2:3], in1=gsub, op0=ALU.mult,
                                op1=ALU.mult), waits=[ast["ot_ev"][ons[qc]], tk_rs], sig=dve_o)
                        ast["o1_free"] = tk_on_list[3]

                    def phase2b(nchk=nchk, ons=ons, tk_on_list=tk_on_list, ms3=ms3, mi=mi, dstm=dstm):
                        tk_last_ev = None
                        for qc in range(4):
                            on = On[ons[qc]]
                            ti = 0
                            pb = PT[ti]
                            w = [tk_on_list[qc], ast.get(("ptev", ti))]
                            tkt = None
                            for a in range(nchk):
                                tkt = B.op("pe", lambda e, a=a, on=on, pb=pb: e.transpose(
                                    out=pb[:, a * 128:(a + 1) * 128], in_=on[:, a * 128:(a + 1) * 128], identity=ident[:]),
                                    waits=w if a == 0 else (), sig=pe_ot)
                            ast["ot_ev"][ons[qc]] = tkt
                            src = pb[:, 0:nchk * 128].rearrange("p (a q) -> p a q", a=nchk)
                            dstv = ms3[:, 0:nchk, qc * 128:(qc + 1) * 128]
                            tk_last_ev = B.op("act", lambda e, src=src, dstv=dstv: e.copy(out=dstv, in_=src),
                                              waits=[tkt, ast["mix_free"][mi]], sig=act_ot)
                            ast[("ptev", ti)] = tk_last_ev
                        ast["mix_free"][mi] = B.dma("sp", dstm, ms3[:, 0:nchk, :], waits=[tk_last_ev], sig=mix_st[mi])

                    if ast.get("pend_a"):
                        ast["pend_a"](); ast["pend_a"] = None
                    if ast.get("pend_b"):
                        ast["pend_b"](); ast["pend_b"] = None
                    ast["pend_a"] = phase2a
                    ast["pend_b"] = phase2b
                    if os.environ.get("NODEFER"):
                        ast["pend_a"](); ast["pend_a"] = None
                        ast["pend_b"](); ast["pend_b"] = None
                    if kind == "d" and ui == 2 * NDH - 1 and g == c.NG - 1:
                        if ast.get("pend_a"):
                            ast["pend_a"](); ast["pend_a"] = None
                        if ast.get("pend_b"):
                            ast["pend_b"](); ast["pend_b"] = None
            ast["kq_free"][b] = last_pe
            if kind == "d" and m == 1:
                ast["vd_free"][vb] = last_pe
            if kind == "f":
                ast["vf_free"][vb] = last_pe
            issue_load(ui + 2)
            if ui == 2 * NDH - 1:
                ast["diff_final"] = ast["o1_free"]
                for du in ast.get("deferred", []):
                    load_vf(du)

        if ast.get("pend_a"):
            ast["pend_a"](); ast["pend_a"] = None
        if ast.get("pend_b"):
            ast["pend_b"](); ast["pend_b"] = None
        B.barrier()

        cc_ = Carver()
        actT = cc_.take(NFC * 512 * 2, BF16)
        U = cc_.take(NCH * 512 * 2, BF16)
        NXS = 4
        NOS = 3
        xst = [cc_.take(512 * 4, F32) for _ in range(NXS)]
        ost = [cc_.take(512 * 4, F32) for _ in range(NOS)]
        sgs = [cc_.take(512 * 4, F32) for _ in range(2)]
        junkF = cc_.take(512 * 4, F32)
        assert cc_.off <= BIG_E * 2, cc_.off
        actT3 = actT.rearrange("p (f t) -> p f t", f=NFC)
        mixT = big[:, 0:NCH * 512]
        mixT3 = mixT.rearrange("p (c t) -> p c t", c=NCH)
        o2 = NCH * 512
        xsC0 = big[:, o2: o2 + 2 * D].bitcast(F32)
        xsC = [xsC0, xsC0]
        o3 = o2 + 2 * D
        hnC = [big[:, o3 + i * D: o3 + (i + 1) * D] for i in range(2)]
        o4 = o3 + 2 * D
        gF = big[:, o4: o4 + 2 * D].bitcast(F32)
        assert o4 + 2 * D <= NFC * 512, "alias overflow"
        h2T = U
        h2T4 = h2T.rearrange("p (c t q) -> p c t q", c=NCH, t=4)
        Ub = NFC * 512
        gFin = big[:, Ub: Ub + 2 * D].bitcast(F32)
        xrow = big[:, Ub + 2 * D: Ub + 4 * D].bitcast(F32)
        assert 4 * D <= NCH * 512

        gF_ld = B.sem("gF_ld")
        mix_ld = B.sem("mix_ld")
        xst_ld = [B.sem(f"xst_ld{i}") for i in range(NXS)]
        ost_st = [B.sem(f"ost_st{i}") for i in range(NOS)]
        pe_acc = B.sem("pe_acc")
        dve_ev = B.sem("dve_ev")
        pe_gu = B.sem("pe_gu")
        act_silu = B.sem("act_silu")
        dve_act = B.sem("dve_act")
        gfin_ld = B.sem("gfin_ld")
        xrow_ld = B.sem("xrow_ld")
        xrow_st = B.sem("xrow_st")
        dve_fin = B.sem("dve_fin")
        cst = {"xs_n": 0, "os_n": 0, "xst_free": [None] * NXS, "ost_free": [None] * NOS, "acc_ev": [None] * 4, "gu_n": 0,
               "gu_ev": {}, "sg_free": [None, None], "fin_tk": None, "xrow_free": None}

        def acc_phase(t, wsrc_fn, nrows_chunks, lhs_fn, lhs_waits, res_dram, dst_dram, after_n=None):
            last = None
            for n in range(D // 512):
                pieces = []
                for k0 in range(0, nrows_chunks, c.KC):
                    pieces.append((k0, min(c.KC, nrows_chunks - k0)))
                tks = [None] * 4
                tkx = [None] * 4
                xis = [None] * 4
                for tb in range(4):
                    r0 = t * 512 + tb * 128
                    xi = cst["xs_n"] % NXS; cst["xs_n"] += 1
                    xis[tb] = xi
                    tkx[tb] = B.dma("sp", xst[xi], res_dram[r0:r0 + 128, n * 512:(n + 1) * 512],
                                    waits=[cst["xst_free"][xi]], sig=xst_ld[xi])
                for pi, (k0, kc) in enumerate(pieces):
                    pv, tkp = req_piece(wsrc_fn(k0, kc, n), (kc, 512))
                    for tb in range(4):
                        for k in range(kc):
                            ch = k0 + k
                            first = (ch == 0); lastk = (ch == nrows_chunks - 1)
                            w = []
                            if k == 0:
                                w = [tkp] + list(lhs_waits)
                                if first:
                                    w.append(cst["acc_ev"][tb])
                            fn = (lambda e, tb=tb, ch=ch, k=k, pv=pv, first=first, lastk=lastk: e.matmul(
                                P[tb][:, :], lhsT=lhs_fn(ch, tb), rhs=pv[:, k, :], start=first, stop=lastk))
                            if lastk or (tb == 3 and k == kc - 1):
                                tk = B.op("pe", fn, waits=w, sig=pe_acc)
                                if lastk:
                                    tks[tb] = tk
                                last = tk
                            else:
                                B.op("pe", fn, waits=w)
                    piece_consumed(last)
                for tb in range(4):
                    r0 = t * 512 + tb * 128
                    xi = xis[tb]
                    oi = cst["os_n"] % NOS; cst["os_n"] += 1
                    tk_e = B.op("dve", lambda e, tb=tb, xi=xi, oi=oi: e.tensor_tensor(out=ost[oi], in0=P[tb][:, :],
                                                                                      in1=xst[xi], op=ALU.add),
                                waits=[tks[tb], tkx[tb], cst["ost_free"][oi]], sig=dve_ev)
                    cst["acc_ev"][tb] = tk_e
                    cst["xst_free"][xi] = tk_e
                    cst["ost_free"][oi] = B.dma("sp", dst_dram[r0:r0 + 128, n * 512:(n + 1) * 512], ost[oi], waits=[tk_e],
                                                sig=ost_st[oi])
                if after_n is not None:
                    after_n(n)
            return last

        pending_final = []

        def run_pending_final(n=None):
            if pending_final:
                pending_final.pop(0)()

        prev_down_last = None
        for t in range(c.NTC):
            B.dma("act", mixT3, mixT_d[:, :, t * 512:(t + 1) * 512].rearrange("c p q -> p c q"), waits=[prev_down_last],
                  sig=mix_ld)
            tk_mix = (mix_ld, mix_ld.n)
            tk_gF = B.dma("act", gF, g_ffn, waits=[prev_down_last], sig=gF_ld)
            last_wo = acc_phase(
                t, (lambda k0, kc, n: w_o[k0 * 128:(k0 + kc) * 128, n * 512:(n + 1) * 512].rearrange("(k p) f -> p k f", p=128)),
                NCH, (lambda ch, tb: mixT3[:, ch, tb * 128:(tb + 1) * 128]), [tk_mix], x_own, x1_d,
                after_n=run_pending_final)
            while pending_final:
                run_pending_final()
            x1_done = [cst["ost_free"][i] for i in range(NOS)]
            tk_h2 = None
            for tb in range(4):
                r0 = t * 512 + tb * 128
                tk_h2 = norm_transpose(xsC, hnC, gF, x1_d[r0:r0 + 128, :],
                                       (lambda c0, n, tb=tb: h2T4[:, c0:c0 + n, tb, :]),
                                       extra_waits=x1_done + [cst["fin_tk"], last_wo], g_wait=tk_gF)
            last_gu = None
            for f0 in range(0, c.DFF, c.PC):
                pg, tkg = req_piece(w_gate[:, f0:f0 + c.PC].rearrange("(c p) f -> p c f", p=128), (NCH, c.PC))
                pu, tku = req_piece(w_up[:, f0:f0 + c.PC].rearrange("(c p) f -> p c f", p=128), (NCH, c.PC))
                lastm = None
                for j in range(c.PC // 128):
                    fi = f0 // 128 + j
                    gi = cst["gu_n"]; cst["gu_n"] += 1
                    bg = P[(gi % 2) * 2]; bu = P[(gi % 2) * 2 + 1]
                    w0 = [tkg, tku, tk_h2, cst["gu_ev"].get(gi - 2)]
                    for (pv, bank, issig) in ((pg, bg, False), (pu, bu, True)):
                        for ch in range(NCH):
                            fn = (lambda e, pv=pv, bank=bank, ch=ch, j=j: e.matmul(
                                bank[:, :], lhsT=pv[:, ch, j * 128:(j + 1) * 128], rhs=h2T[:, ch * 512:(ch + 1) * 512],
                                start=(ch == 0), stop=(ch == NCH - 1)))
                            if issig and ch == NCH - 1:
                                lastm = B.op("pe", fn, sig=pe_gu)
                            else:
                                B.op("pe", fn, waits=w0 if ch == 0 else ())
                    si = gi % 2
                    tk_s = B.op("act", lambda e, bg=bg, si=si: e.activation(out=sgs[si], in_=bg[:, :], func=AF.Silu),
                                waits=[lastm, cst["sg_free"][si]], sig=act_silu)
                    tk_a = B.op("dve", lambda e, bu=bu, si=si, fi=fi: e.tensor_tensor(out=actT3[:, fi, :], in0=sgs[si],
                                                                                      in1=bu[:, :], op=ALU.mult),
                                waits=[tk_s], sig=dve_act)
                    cst["sg_free"][si] = tk_a
                    cst["gu_ev"][gi] = tk_a
                piece_consumed(lastm)
                piece_consumed(lastm)
                last_gu = lastm
            tk_act_done = cst["gu_ev"][cst["gu_n"] - 1]
            last_dn = acc_phase(
                t, (lambda k0, kc, n: w_down[k0 * 128:(k0 + kc) * 128, n * 512:(n + 1) * 512].rearrange("(k p) f -> p k f", p=128)),
                NFC, (lambda ch, tb: actT3[:, ch, tb * 128:(tb + 1) * 128]), [tk_act_done], x1_d, out_d)
            prev_down_last = last_dn
            x2_done = [cst["ost_free"][i] for i in range(NOS)]
            def final_block(tb, t=t, x2_done=x2_done, last_gu=last_gu):
                if tb == 0:
                    cst["gfin_tk"] = B.dma("act", gFin, g_fin, waits=[last_gu], sig=gfin_ld)
                tk_gfin = cst["gfin_tk"]
                r0 = t * 512 + tb * 128
                tk_r = B.dma("act", xrow, out_d[r0:r0 + 128, :], waits=x2_done + [cst["xrow_free"], last_gu], sig=xrow_ld)
                B.op("dve", lambda e: e.memset(ssb[:, 3:4], 0.0), waits=[tk_r])
                for q8 in range(D // 512):
                    B.op("dve", lambda e, q8=q8: e.scalar_tensor_tensor(
                        out=junkF, in0=xrow[:, q8 * 512:(q8 + 1) * 512], scalar=1.0, in1=xrow[:, q8 * 512:(q8 + 1) * 512],
                        op0=ALU.mult, op1=ALU.mult, accum_out=rlb[:, q8 % 8:q8 % 8 + 1]))
                    B.op("dve", lambda e, q8=q8: e.tensor_tensor(out=ssb[:, 3:4], in0=ssb[:, 3:4],
                                                                 in1=rlb[:, q8 % 8:q8 % 8 + 1], op=ALU.add))
                tk_ss = B.op("dve", lambda e: e.tensor_copy(out=ssb[:, 3:4], in_=ssb[:, 3:4]), sig=dve_ss)
                tk_rs = rstd_act(ssb[:, 3:4], rsb[:, 3:4], 1.0 / D, tk_ss)
                tk_fn = B.op("dve", lambda e: e.scalar_tensor_tensor(out=xrow, in0=xrow, scalar=rsb[:, 3:4], in1=gFin,
                                                                     op0=ALU.mult, op1=ALU.mult),
                             waits=[tk_gfin, tk_rs], sig=dve_fin)
                cst["xrow_free"] = B.dma("act", out_d[r0:r0 + 128, :], xrow, waits=[tk_fn], sig=xrow_st)
                cst["fin_tk"] = cst["xrow_free"]

            for tb in range(4):
                pending_final.append(lambda tb=tb, fb=final_block: fb(tb))

        while pending_final:
            run_pending_final()

        B.barrier()
        B.finish()
    return nc


def _bucket(n):
    max_exact = NUM_BUCKETS // 2
    nf = np.maximum(n, 1).astype(np.float32)
    large = max_exact + (np.log(nf / np.float32(max_exact)) / np.float32(math.log(MAX_DISTANCE / max_exact))
                         * np.float32(NUM_BUCKETS - max_exact)).astype(np.int32)
    large = np.minimum(large, NUM_BUCKETS - 1)
    return np.where(n < max_exact, n, large)


def _bucket_jax(n):
    import jax
    import jax.numpy as jnp
    with jax.default_device(jax.devices("cpu")[0]):
        max_exact = NUM_BUCKETS // 2
        nn = jnp.asarray(n, dtype=jnp.int32)
        nf = jnp.maximum(nn, 1).astype(jnp.float32)
        large = max_exact + (jnp.log(nf / max_exact) / math.log(MAX_DISTANCE / max_exact)
                             * (NUM_BUCKETS - max_exact)).astype(jnp.int32)
        large = jnp.minimum(large, NUM_BUCKETS - 1)
        return np.asarray(jnp.where(nn < max_exact, nn, large))


def core_consts(r):
    p = np.arange(128)
    glob = np.where(p < 64, 64 * r + p, 64 * (1 - r) + (p - 64))
    tri = (glob[:, None] <= glob[None, :]).astype(np.float32)
    ohm = np.zeros((128, NUM_BUCKETS + 1, 128), np.float32)
    qq = np.arange(64)
    try:
        bfun = _bucket_jax
        bfun(np.arange(4))
    except Exception:
        bfun = _bucket
    for u in range(2):
        dist = 128 * u + (64 * r + qq)[None, :] - glob[:, None]
        bk = bfun(np.maximum(dist, 0))
        for b in range(NUM_BUCKETS):
            ohm[:, b, u * 64:(u + 1) * 64] = ((bk == b) & (dist >= 0)).astype(np.float32)
        ohm[:, NUM_BUCKETS, u * 64:(u + 1) * 64] = np.where(dist >= 0, 0.0, NEG)
    return tri, ohm.reshape(128, -1)


def make_in_maps(cfg, inputs):
    c = cfg
    x = np.asarray(inputs["x"], np.float32)
    nb = x.shape[0]
    f = lambda k: np.ascontiguousarray(np.asarray(inputs[k], np.float32))
    w_in = f("w_in")[0]; w_o = f("w_o")[0]; w_gate = f("w_gate")[0]; w_up = f("w_up")[0]; w_down = f("w_down")[0]
    bc = lambda v: np.ascontiguousarray(np.broadcast_to(np.asarray(v, np.float32).reshape(1, -1), (128, v.size)))
    g_attn = bc(f("attn_norm_g")[0]); g_ffn = bc(f("ffn_norm_g")[0]); g_fin = bc(f("final_norm_g"))
    g_sub = bc(f("diff_subln_g")[0]); bf_bc = bc(f("b_f")[0])
    lam = np.concatenate([f("lambda_q1")[0], f("lambda_k1")[0], f("lambda_q2")[0], f("lambda_k2")[0]])
    lam_bc = bc(lam)
    tab_bc = bc(f("rel_bias_table").reshape(-1))
    sel = np.zeros((128, c.NFH, 128), np.float32)
    for h in range(c.NFH):
        sel[h, h, :] = 1.0
    sel = sel.reshape(128, -1)
    ident = np.eye(128, dtype=np.float32)
    consts = [core_consts(0), core_consts(1)]
    in_maps = []
    perms = []
    for core in range(2 * nb):
        b, r = core // 2, core % 2
        xb = x[b].reshape(c.NB, 2, 64, c.D)
        x_loc = np.ascontiguousarray(np.stack([xb[:, r], xb[:, 1 - r]], axis=1).reshape(c.S, c.D))
        x_own = np.ascontiguousarray(xb[:, r].reshape(c.NOWN, c.D))
        tri, ohm = consts[r]
        in_maps.append(dict(x_loc=x_loc, x_own=x_own, w_in=w_in, w_o=w_o, w_gate=w_gate, w_up=w_up, w_down=w_down,
                            g_attn=g_attn, g_ffn=g_ffn, g_fin=g_fin, g_sub=g_sub, bf_bc=bf_bc, lam_bc=lam_bc,
                            tab_bc=tab_bc, tri=tri, ohm=ohm, sel=sel, ident=ident))
    return in_maps


def assemble(cfg, results, nb):
    c = cfg
    out = np.empty((nb, c.NB, 2, 64, c.D), np.float32)
    for core in range(2 * nb):
        b, r = core // 2, core % 2
        out[b, :, r] = np.asarray(results[core]["out"]).reshape(c.NB, 64, c.D)
    return out.reshape(nb, c.S, c.D)


_CACHE = {}


def kernel(**inputs):
    cfg = Cfg()
    if "nc" not in _CACHE:
        _CACHE["nc"] = build(cfg)
    nc = _CACHE["nc"]
    in_maps = make_in_maps(cfg, inputs)
    nb = np.asarray(inputs["x"]).shape[0]
    res = run_bass_kernel_spmd(nc, in_maps, core_ids=list(range(2 * nb)))
    return assemble(cfg, res.results, nb)
```
